# Optimizing a Trainium2 kernel written in Bass

```python
import jax, jax.numpy as jnp
from jax import lax
import numpy as np

D_MODEL = 2048
BATCH = 4
SEQ = 2048
DEPTH = 1
DEC_BATCH = 128
DEC_SEQ = 1
PAST_LEN = 16384
PAGE_SIZE = 128

D_PLE = 256
D_A = D_MODEL // 2
HD_A = 64
H_A = D_A // HD_A
LORA_W = 64
LORA_A = 64
D_B = D_MODEL - D_A
H_B = 4
HD_B = D_B // H_B
CHUNK = 128
ROPE_BASE = 10000.0
RMS_EPS = 1e-6
GN_EPS_A = HD_A * 1e-5
GN_EPS_B = 1e-5

A_SHIFT = 3 * D_A + LORA_W + LORA_A
OFF_GA = A_SHIFT
OFF_B = A_SHIFT + D_A
N_COLS = OFF_B + 4 * D_B

kernel_name = "hybrid_rwkv7_retention_decode_step"


def rmsnorm(x, g):
    xf = x.astype(jnp.float32)
    y = xf * lax.rsqrt(jnp.mean(xf * xf, axis=-1, keepdims=True) + RMS_EPS)
    return (y * g.astype(jnp.float32)).astype(x.dtype)


def head_norm(x, g, b, eps):
    mu = jnp.mean(x, axis=-1, keepdims=True)
    xc = x - mu
    var = jnp.mean(xc * xc, axis=-1, keepdims=True)
    y = (xc * lax.rsqrt(var + eps)).reshape(x.shape[:-2] + (-1,))
    return y * g.astype(jnp.float32) + b.astype(jnp.float32)


def rotary_every_two(x, pos):
    d = x.shape[-1]
    angle = 1.0 / (ROPE_BASE ** jnp.linspace(0.0, 1.0, d // 2, dtype=jnp.float32))
    theta = pos[:, None] * angle[None, :]
    cos = jnp.cos(theta)[None, :, None, :]
    sin = jnp.sin(theta)[None, :, None, :]
    xp = x.reshape(x.shape[:-1] + (d // 2, 2))
    x1, x2 = xp[..., 0], xp[..., 1]
    out = jnp.stack([x1 * cos - x2 * sin, x1 * sin + x2 * cos], axis=-1)
    return out.reshape(x.shape)


def rwkv7_scan(S0, r, w, k, v, a_vec, b_vec):
    def step(S, inp):
        r_t, w_t, k_t, v_t, a_t, b_t = inp
        sa = jnp.einsum('bhvk,bhk->bhv', S, a_t)
        S = S * w_t[:, :, None, :] + sa[..., None] * b_t[:, :, None, :] + v_t[..., None] * k_t[:, :, None, :]
        y = jnp.einsum('bhvk,bhk->bhv', S, r_t)
        return S, y
    xs = tuple(jnp.swapaxes(t, 0, 1) for t in (r, w, k, v, a_vec, b_vec))
    S_T, ys = lax.scan(step, S0, xs)
    return jnp.swapaxes(ys, 0, 1), S_T


def retention_chunk(S0, q, k, v, lg):
    L = q.shape[2]
    idx = jnp.arange(L, dtype=jnp.float32)
    diff = idx[:, None] - idx[None, :]
    dmask = jnp.where(diff >= 0, jnp.exp(jnp.maximum(diff, 0.0)[None] * lg[:, None, None]), 0.0)
    scores = jnp.einsum('bhid,bhjd->bhij', q, k) * dmask[None]
    inner = jnp.einsum('bhij,bhjv->bhiv', scores, v)
    cross = jnp.einsum('bhid,bhdv->bhiv', q, S0) * jnp.exp((idx + 1.0)[None, :] * lg[:, None])[None, :, :, None]
    kdec = jnp.exp((L - 1.0 - idx)[None, :] * lg[:, None])
    S_new = S0 * jnp.exp(L * lg)[None, :, None, None] + jnp.einsum('bhjd,bhjv,hj->bhdv', k, v, kdec)
    return inner + cross, S_new


def retention(S0, q, k, v, lg):
    B, H, T, _ = q.shape
    chunk = CHUNK if T % CHUNK == 0 else T
    nc = T // chunk
    def to_chunks(t):
        return jnp.moveaxis(t.reshape(B, H, nc, chunk, t.shape[-1]), 2, 0)
    def step(S, inp):
        qc, kc, vc = inp
        o, S = retention_chunk(S, qc, kc, vc, lg)
        return S, o
    S_T, os_ = lax.scan(step, S0, (to_chunks(q), to_chunks(k), to_chunks(v)))
    out = jnp.moveaxis(os_, 0, 2).reshape(B, H, T, v.shape[-1])
    return out, S_T


def hybrid_layer(h, p, pos, wkv0, shift0, ret0, g_ln, w_in, mu_shift, w0, w_wB, a0, w_aB, k_k, k_a, r_k,
                 gn_a_g, gn_a_b, gn_b_g, gn_b_b, w_out, w_ple, w_ple_gate):
    B, T, _ = h.shape
    f32 = jnp.float32
    u = rmsnorm(h, g_ln)
    z = u @ w_in

    feats = z[..., :A_SHIFT]
    prev = jnp.concatenate([shift0[:, None].astype(feats.dtype), feats[:, :-1]], axis=1)
    xs = (feats + mu_shift * (prev - feats)).astype(f32)
    r = xs[..., :D_A]
    k = xs[..., D_A:2 * D_A]
    v = xs[..., 2 * D_A:3 * D_A]
    w_lo = xs[..., 3 * D_A:3 * D_A + LORA_W]
    a_lo = xs[..., 3 * D_A + LORA_W:]
    w_log = -jax.nn.softplus(-(w0.astype(f32) + jnp.tanh(w_lo) @ w_wB.astype(f32))) - 0.5
    decay = jnp.exp(-jnp.exp(w_log))
    a = jax.nn.sigmoid(a0.astype(f32) + a_lo @ w_aB.astype(f32))
    hs = lambda t: t.reshape(B, T, H_A, HD_A)
    kk = hs(k * k_k.astype(f32))
    kk = kk / jnp.maximum(jnp.sqrt(jnp.sum(kk * kk, axis=-1, keepdims=True)), 1e-12)
    k = k * (1.0 + (a - 1.0) * k_a.astype(f32))
    r4, k4, v4, a4 = hs(r), hs(k), hs(v), hs(a)
    y_a, wkv_T = rwkv7_scan(wkv0.astype(f32), r4, hs(decay), k4, v4, -kk, kk * a4)
    bonus = (jnp.sum(r4 * k4 * r_k.astype(f32), axis=-1, keepdims=True) * v4).reshape(B, T, D_A)
    y_a = (head_norm(y_a, gn_a_g, gn_a_b, GN_EPS_A) + bonus) * jax.nn.silu(z[..., OFF_GA:OFF_GA + D_A].astype(f32))

    hb = lambda t: t.astype(f32).reshape(B, T, H_B, HD_B)
    qb = rotary_every_two(hb(z[..., OFF_B:OFF_B + D_B]), pos)
    kb = rotary_every_two(hb(z[..., OFF_B + D_B:OFF_B + 2 * D_B]), pos) * (HD_B ** -0.5)
    vb = hb(z[..., OFF_B + 2 * D_B:OFF_B + 3 * D_B])
    gb = z[..., OFF_B + 3 * D_B:].astype(f32)
    lg = jnp.log(1.0 - jnp.exp2(-5.0 - jnp.arange(H_B, dtype=f32)))
    to_bh = lambda t: jnp.transpose(t, (0, 2, 1, 3))
    y_b, ret_T = retention(ret0.astype(f32), to_bh(qb), to_bh(kb), to_bh(vb), lg)
    y_b = head_norm(to_bh(y_b), gn_b_g, gn_b_b, GN_EPS_B) * jax.nn.silu(gb)

    mix = jnp.concatenate([y_a, y_b], axis=-1).astype(h.dtype)
    h = h + mix @ w_out
    h = h + jax.nn.sigmoid(h @ w_ple_gate) * (p @ w_ple)
    return h, wkv_T, feats[:, -1], ret_T


def setup_inputs(seed: int = 0) -> dict:
    key = jax.random.key(seed)
    ks = jax.random.split(key, 32)
    f32 = jnp.float32
    nrm = lambda k, s, sc: jax.random.normal(k, s, f32) * sc
    return {
        "x_prompt": nrm(ks[0], (BATCH, SEQ, D_MODEL), 1.0),
        "x_sample": nrm(ks[1], (DEC_BATCH, DEC_SEQ, D_MODEL), 1.0),
        "p_prompt": nrm(ks[2], (DEPTH, BATCH, SEQ, D_PLE), 1.0),
        "p_sample": nrm(ks[3], (DEPTH, DEC_BATCH, DEC_SEQ, D_PLE), 1.0),
        "state_wkv": nrm(ks[4], (DEPTH, DEC_BATCH, H_A, HD_A, HD_A), 0.5),
        "state_shift": nrm(ks[5], (DEPTH, DEC_BATCH, A_SHIFT), 1.0),
        "state_ret": nrm(ks[6], (DEPTH, DEC_BATCH, H_B, HD_B, HD_B), 1.0),
        "g_ln": 1.0 + nrm(ks[7], (DEPTH, D_MODEL), 0.02),
        "w_in": nrm(ks[8], (DEPTH, D_MODEL, N_COLS), D_MODEL ** -0.5),
        "mu_shift": jax.random.uniform(ks[9], (DEPTH, A_SHIFT), f32, 0.0, 1.0),
        "w0": jax.random.uniform(ks[10], (DEPTH, D_A), f32, -5.0, 0.0),
        "w_wB": nrm(ks[11], (DEPTH, LORA_W, D_A), 0.1 * LORA_W ** -0.5),
        "a0": nrm(ks[12], (DEPTH, D_A), 0.1),
        "w_aB": nrm(ks[13], (DEPTH, LORA_A, D_A), 0.1 * LORA_A ** -0.5),
        "k_k": 0.85 + nrm(ks[14], (DEPTH, D_A), 0.02),
        "k_a": 1.0 + nrm(ks[15], (DEPTH, D_A), 0.02),
        "r_k": nrm(ks[16], (DEPTH, H_A, HD_A), 0.1),
        "gn_a_g": 1.0 + nrm(ks[17], (DEPTH, D_A), 0.02),
        "gn_a_b": nrm(ks[18], (DEPTH, D_A), 0.02),
        "gn_b_g": 1.0 + nrm(ks[19], (DEPTH, D_B), 0.02),
        "gn_b_b": nrm(ks[20], (DEPTH, D_B), 0.02),
        "w_out": nrm(ks[21], (DEPTH, D_A + D_B, D_MODEL), (D_A + D_B) ** -0.5),
        "w_ple": nrm(ks[22], (DEPTH, D_PLE, D_MODEL), D_PLE ** -0.5),
        "w_ple_gate": nrm(ks[23], (DEPTH, D_MODEL, D_MODEL), D_MODEL ** -0.5),
        "g_final": 1.0 + nrm(ks[24], (D_MODEL,), 0.02),
    }


def reference(x_prompt, x_sample, p_prompt, p_sample, state_wkv, state_shift, state_ret, g_ln, w_in, mu_shift,
              w0, w_wB, a0, w_aB, k_k, k_a, r_k, gn_a_g, gn_a_b, gn_b_g, gn_b_b, w_out, w_ple, w_ple_gate, g_final):
    f32 = jnp.float32
    Bp, Tp, _ = x_prompt.shape
    Bs, Ts, _ = x_sample.shape
    pos_p = jnp.arange(Tp, dtype=f32)
    pos_s = (PAST_LEN + jnp.arange(Ts)).astype(f32)
    hp, hs_ = x_prompt, x_sample
    wkv_p, shift_p, ret_p, wkv_s, shift_s, ret_s = [], [], [], [], [], []
    for i in range(DEPTH):
        lw = (g_ln[i], w_in[i], mu_shift[i], w0[i], w_wB[i], a0[i], w_aB[i], k_k[i], k_a[i], r_k[i],
              gn_a_g[i], gn_a_b[i], gn_b_g[i], gn_b_b[i], w_out[i], w_ple[i], w_ple_gate[i])
        hp, s1, s2, s3 = hybrid_layer(hp, p_prompt[i], pos_p,
                                      jnp.zeros((Bp, H_A, HD_A, HD_A), f32),
                                      jnp.zeros((Bp, A_SHIFT), x_prompt.dtype),
                                      jnp.zeros((Bp, H_B, HD_B, HD_B), f32), *lw)
        wkv_p.append(s1); shift_p.append(s2); ret_p.append(s3)
        hs_, s1, s2, s3 = hybrid_layer(hs_, p_sample[i], pos_s, state_wkv[i], state_shift[i], state_ret[i], *lw)
        wkv_s.append(s1); shift_s.append(s2); ret_s.append(s3)
    y_prompt = rmsnorm(hp, g_final)
    y_sample = rmsnorm(hs_, g_final)
    return (y_prompt, y_sample, jnp.stack(wkv_p), jnp.stack(shift_p), jnp.stack(ret_p),
            jnp.stack(wkv_s), jnp.stack(shift_s), jnp.stack(ret_s))
```

```python
import os
from contextlib import ExitStack
import numpy as np
import concourse.bass as bass
import concourse.mybir as mybir
from concourse.bass_utils import run_bass_kernel_spmd


F32 = mybir.dt.float32
BF16 = mybir.dt.bfloat16
U8 = mybir.dt.uint8
AF = mybir.ActivationFunctionType
ALU = mybir.AluOpType
DT_SIZE = {F32: 4, BF16: 2, U8: 1}

ENGS = ("pe", "act", "dve", "pool", "sp")


class Tile:
    __slots__ = ("ap", "name", "writers", "readers")

    def __init__(self, ap, name):
        self.ap = ap
        self.name = name
        self.writers = {}
        self.readers = {}

    def __getitem__(self, k):
        return V(self, self.ap[k])

    def re(self, s, **kw):
        return V(self, self.ap.rearrange(s, **kw))


class V:
    __slots__ = ("t", "ap")

    def __init__(self, t, ap):
        self.t = t
        self.ap = ap

    def __getitem__(self, k):
        return V(self.t, self.ap[k])

    def re(self, s, **kw):
        return V(self.t, self.ap.rearrange(s, **kw))

    def bc(self, shape):
        return V(self.t, self.ap.to_broadcast(list(shape)))

    def un(self, axis):
        return V(self.t, self.ap.unsqueeze(axis))


def _ap(x):
    return x.ap if isinstance(x, (V, Tile)) else x


def _tl(x):
    if isinstance(x, V):
        return x.t
    if isinstance(x, Tile):
        return x
    return None


class Sched:
    def __init__(self, nc):
        self.nc = nc
        self.q = {e: [] for e in ENGS}
        self.cnt = {e: 0 for e in ENGS}
        self.dcnt = {}
        self.seen = {e: {} for e in ENGS}
        self.nops = 0

    def _deps(self, eng, reads, writes):
        own = "c_" + eng
        n = self.cnt[eng]
        need = {}

        def add(k, v):
            if k == own:
                if eng in ("pe", "sp"):
                    return
            if need.get(k, 0) < v:
                need[k] = v

        for t in reads:
            for k, v in t.writers.items():
                add(k, v)
        for t in writes:
            for k, v in t.writers.items():
                add(k, v)
            for k, v in t.readers.items():
                add(k, v)
        waits = []
        seen = self.seen[eng]
        for k, v in need.items():
            if seen.get(k, 0) >= v:
                continue
            seen[k] = v
            waits.append((k, v))
        return waits

    def _mark(self, tok, reads, writes):
        k, v = tok
        for t in writes:
            t.writers = {k: v}
            t.readers = {}
        for t in reads:
            if t in writes:
                continue
            if t.readers.get(k, 0) < v:
                t.readers[k] = v

    def op(self, eng, fn, ins=(), outs=()):
        reads = [t for t in (_tl(x) for x in ins) if t is not None]
        writes = [t for t in (_tl(x) for x in outs) if t is not None]
        waits = self._deps(eng, reads, writes)
        self.cnt[eng] += 1
        tok = ("c_" + eng, self.cnt[eng])
        self.q[eng].append((waits, fn, tok[0], 1))
        self._mark(tok, reads, writes)
        self.nops += 1

    def dma(self, eng, out, in_, stream="d0", **kw):
        reads = [t for t in [_tl(in_)] if t is not None]
        writes = [t for t in [_tl(out)] if t is not None]
        waits = self._deps(eng, reads, writes)
        key = "d_" + stream
        prev = self.dcnt.get(key, 0)
        if prev > 0 and self.seen[eng].get(key, 0) < prev:
            self.seen[eng][key] = prev
            waits = [w for w in waits if w[0] != key] + [(key, prev)]
        self.dcnt[key] = prev + 16
        tok = (key, self.dcnt[key])
        o, i = _ap(out), _ap(in_)
        self.q[eng].append((waits, lambda e: e.dma_start(out=o, in_=i, **kw), key, 16))
        self._mark(tok, reads, writes)
        self.nops += 1
        return tok

    def custom(self, eng, fn, key, inc, ins=(), outs=()):
        reads = [t for t in (_tl(x) for x in ins) if t is not None]
        writes = [t for t in (_tl(x) for x in outs) if t is not None]
        waits = self._deps(eng, reads, writes)
        self.dcnt[key] = self.dcnt.get(key, 0) + inc
        tok = (key, self.dcnt[key])
        self.q[eng].append((waits, fn, key, inc))
        self._mark(tok, reads, writes)

    def barrier(self):
        allk = {("c_" + e): self.cnt[e] for e in ENGS if self.cnt[e] > 0}
        allk.update(self.dcnt)
        for e in ENGS:
            waits = []
            for k, v in allk.items():
                if k == "c_" + e and e in ("pe", "sp"):
                    continue
                if k != "c_" + e and self.seen[e].get(k, 0) >= v:
                    continue
                self.seen[e][k] = v
                waits.append((k, v))
            if waits:
                self.q[e].append((waits, None, None, 0))

    def mm(self, out, lhsT, rhs, start=True, stop=True, **kw):
        o, l, r = _ap(out), _ap(lhsT), _ap(rhs)
        self.op("pe", lambda e: e.matmul(o, l, r, start=start, stop=stop, **kw),
                ins=(lhsT, rhs), outs=(out,))

    def act(self, out, in_, func, bias=0.0, scale=1.0, eng="act", extra_ins=()):
        o, i = _ap(out), _ap(in_)
        b, s = _ap(bias), _ap(scale)
        self.op(eng, lambda e: e.activation(out=o, in_=i, func=func, bias=b, scale=s),
                ins=(in_, bias, scale) + tuple(extra_ins), outs=(out,))

    def tt(self, out, in0, in1, op, eng="dve"):
        o, a, b = _ap(out), _ap(in0), _ap(in1)
        self.op(eng, lambda e: e.tensor_tensor(out=o, in0=a, in1=b, op=op), ins=(in0, in1), outs=(out,))

    def ts(self, out, in0, s1, s2, op0, op1=None, eng="dve"):
        o, a = _ap(out), _ap(in0)
        x1, x2 = _ap(s1), _ap(s2)
        if op1 is None:
            self.op(eng, lambda e: e.tensor_scalar(out=o, in0=a, scalar1=x1, scalar2=None, op0=op0),
                    ins=(in0, s1), outs=(out,))
        else:
            self.op(eng, lambda e: e.tensor_scalar(out=o, in0=a, scalar1=x1, scalar2=x2, op0=op0, op1=op1),
                    ins=(in0, s1, s2), outs=(out,))

    def stt(self, out, in0, scalar, in1, op0, op1, eng="dve"):
        o, a, b = _ap(out), _ap(in0), _ap(in1)
        s = _ap(scalar)
        self.op(eng, lambda e: e.scalar_tensor_tensor(out=o, in0=a, scalar=s, in1=b, op0=op0, op1=op1),
                ins=(in0, scalar, in1), outs=(out,))

    def copy(self, out, in_, eng="dve"):
        o, i = _ap(out), _ap(in_)
        if eng == "act":
            self.op(eng, lambda e: e.activation(out=o, in_=i, func=AF.Copy), ins=(in_,), outs=(out,))
        else:
            self.op(eng, lambda e: e.tensor_copy(out=o, in_=i), ins=(in_,), outs=(out,))

    def recip(self, out, in_):
        o, i = _ap(out), _ap(in_)
        self.op("dve", lambda e: e.reciprocal(out=o, in_=i), ins=(in_,), outs=(out,))

    def memset(self, out, val, eng="pool"):
        o = _ap(out)
        self.op(eng, lambda e: e.memset(o, val), outs=(out,))

    def emit(self):
        nc = self.nc
        self.barrier()
        keys = set(self.dcnt.keys()) | {"c_" + e for e in ENGS}
        with ExitStack() as es:
            sems = {k: es.enter_context(nc.semaphore(k)) for k in sorted(keys)}
            block = es.enter_context(nc.Block())

            def replay(name):
                def f(e):
                    for waits, fn, key, inc in self.q[name]:
                        for k, v in waits:
                            e.wait_ge(sems[k], v)
                        if fn is None:
                            continue
                        ins = fn(e)
                        ins.then_inc(sems[key], inc)
                return f

            block.tensor(replay("pe"))
            block.scalar(replay("act"))
            block.vector(replay("dve"))
            block.gpsimd(replay("pool"))
            block.sync(replay("sp"))


class Arena:
    def __init__(self, ap, nbytes, name):
        self.base = ap
        self.n = nbytes
        self.off = 0
        self.name = name
        self.marks = []

    def tile(self, shape, dtype, name):
        sz = DT_SIZE[dtype]
        free = int(np.prod(shape[1:])) * sz
        self.off = (self.off + 31) // 32 * 32
        assert self.off + free <= self.n, f"arena {self.name} overflow at {name}: {self.off}+{free}>{self.n}"
        v = self.base[:, self.off:self.off + free].bitcast(dtype)
        self.off += free
        if len(shape) > 2:
            names = " ".join(f"a{i}" for i in range(len(shape) - 1))
            kw = {f"a{i}": shape[i + 1] for i in range(len(shape) - 1)}
            v = v.rearrange(f"p ({names}) -> p {names}", **kw)
        if shape[0] < 128:
            v = v[0:shape[0]]
        return Tile(v, name)

    def mark(self):
        self.marks.append(self.off)

    def release(self):
        self.off = self.marks.pop()


D = 2048; T = 2048; NS = 32; NT = T + NS; TH = 1024; NSH = 16; NTH = TH + NSH
D_A = 1024; D_B = 1024; HD_B = 256
A_SHIFT = 3 * D_A + 128
OFF_GA = A_SHIFT
OFF_B = A_SHIFT + D_A
NPC = 94
POS_S = 16384.0

C_ID = 0; C_M1 = 128; C_MSL = 384; C_BONES = 512; C_ID64 = 640; C_RESET = 704
C_DT = 1216; C_GQ = 1472; C_KDEC = 2496; C_G128 = 2498; C_GAM = 2500; NCST = 2502


def col_chunks(hh):
    ch = []
    ch.append(3 * D_A + np.arange(128))
    for j in range(4):
        pc = 64 * (8 * hh + 2 * j) + np.arange(128)
        ch.append(D_A + pc)
        ch.append(0 * D_A + pc)
        ch.append(2 * D_A + pc)
        ch.append(OFF_GA + pc)
    for hb in range(2):
        H = 2 * hh + hb
        ev = 256 * H + 2 * np.arange(128)
        od = ev + 1
        ch.append(OFF_B + ev)
        ch.append(OFF_B + od)
        ch.append(OFF_B + D_B + ev)
        ch.append(OFF_B + D_B + od)
        ch.append(OFF_B + 3 * D_B + 256 * H + np.arange(128))
        ch.append(OFF_B + 3 * D_B + 256 * H + 128 + np.arange(128))
        ch.append(OFF_B + 2 * D_B + 256 * H + np.arange(128))
        ch.append(OFF_B + 2 * D_B + 256 * H + 128 + np.arange(128))
    return ch


def shift_chunk_cols(hh):
    ch = col_chunks(hh)
    out = [ch[0]]
    for j in range(4):
        out += [ch[1 + 4 * j], ch[2 + 4 * j], ch[3 + 4 * j]]
    return out


def mix_rows(hh):
    rows = []
    for j in range(4):
        rows.append(64 * (8 * hh + 2 * j) + np.arange(128))
    for hb in range(2):
        H = 2 * hh + hb
        for vc in range(2):
            rows.append(D_A + 256 * H + 128 * vc + np.arange(128))
    return np.concatenate(rows)


def tile_w(w, rows_chunked=16):
    K, n = w.shape
    kc = K // 128
    t = w.reshape(kc, 128, n // 128, 128)
    return np.ascontiguousarray(t.transpose(2, 1, 0, 3)).reshape(n // 128, 128, kc * 128)


def consts(hh):
    c = np.zeros((128, NCST), np.float32)
    p = np.arange(128)[:, None]
    f = np.arange(128)[None, :]
    c[:, C_ID:C_ID + 128] = (p == f)
    c[:, C_M1:C_M1 + 128] = (p < f)
    c[:, C_M1 + 128:C_M1 + 256] = (p <= f)
    c[:, C_MSL:C_MSL + 128] = (f < p)
    c[:, C_BONES:C_BONES + 128] = ((p // 64) == (f // 64))
    c[:, C_ID64:C_ID64 + 64] = ((p % 64) == np.arange(64)[None, :])
    c[:, C_RESET:C_RESET + 512] = ((np.arange(512) % 128) != 0)[None, :]
    for hb in range(2):
        H = 2 * hh + hb
        gam = np.float64(1.0) - np.exp2(np.float64(-5.0 - H))
        gam = np.float64(np.float32(gam))
        lg = np.log(gam)
        c[:, C_DT + 128 * hb:C_DT + 128 * hb + 128] = np.where(p <= f, np.exp(-(p + 1.0) * lg), 0.0)
        c[:, C_GQ + 512 * hb:C_GQ + 512 * hb + 512] = np.exp(((np.arange(512) % 128) + 1.0) * lg)[None, :]
        c[:, C_KDEC + hb] = np.exp((127.0 - np.arange(128)) * lg)
        c[:, C_G128 + hb] = np.exp(128.0 * lg)
        c[:, C_GAM + hb] = gam
    return c


def rot_table():
    ang = (1.0 / (np.float32(10000.0) ** np.linspace(0.0, 1.0, 128, dtype=np.float32))).astype(np.float32)
    pos = np.concatenate([np.arange(T, dtype=np.float32), np.full(NS, POS_S, np.float32)])
    th = (pos[None, :] * ang[:, None]).astype(np.float32)
    th64 = th.astype(np.float64)
    r = np.zeros((128, 2, NT), np.float32)
    r[:, 0] = np.cos(th64)
    r[:, 1] = np.sin(th64)
    return r


def prep_inputs(inp):
    f = lambda a: np.ascontiguousarray(a, dtype=np.float32)
    w_in = np.asarray(inp["w_in"])[0]
    w_out = np.asarray(inp["w_out"])[0]
    w_gate = np.asarray(inp["w_ple_gate"])[0]
    w_ple = np.asarray(inp["w_ple"])[0]
    xp = np.asarray(inp["x_prompt"]); xs = np.asarray(inp["x_sample"])[:, 0]
    pp = np.asarray(inp["p_prompt"])[0]; ps = np.asarray(inp["p_sample"])[0][:, 0]
    swkv = np.asarray(inp["state_wkv"])[0]; sshift = np.asarray(inp["state_shift"])[0]
    sret = np.asarray(inp["state_ret"])[0]
    rot = rot_table()
    vec = {k: np.asarray(inp[k])[0] for k in ("g_ln", "mu_shift", "w0", "a0", "k_k", "k_a", "gn_a_g", "gn_a_b", "gn_b_g", "gn_b_b")}
    r_k = np.asarray(inp["r_k"])[0].reshape(-1)
    w_wB = np.asarray(inp["w_wB"])[0]; w_aB = np.asarray(inp["w_aB"])[0]
    g_final = np.asarray(inp["g_final"])
    mrows = np.concatenate([mix_rows(0), mix_rows(1)])
    w2 = np.concatenate([tile_w(w_out[mrows, :]), tile_w(w_gate)], axis=0)
    w3 = tile_w(w_ple)
    per_core = []
    for c in range(8):
        b, hh = c // 2, c % 2
        d = {}
        sb = slice(32 * b, 32 * b + 32)
        xpt = xp[b].T.reshape(16, 128, 8, 256).transpose(2, 1, 0, 3)
        d["xTp"] = f(xpt.reshape(8, 128, 16 * 256))
        d["xTs"] = f(xs[sb].T.reshape(16, 128, NS).transpose(1, 0, 2).reshape(128, 16 * NS))
        chs = col_chunks(hh)
        d["w1"] = f(tile_w(w_in[:, np.concatenate(chs)]))
        d["w2"] = f(w2)
        d["w3"] = f(w3)
        acols = 64 * 8 * hh + np.arange(512)
        d["wB"] = f(np.concatenate([w_wB[:, acols], w_aB[:, acols]], axis=0))
        pc = np.zeros((128, NPC), np.float32)
        pc[:, 0:16] = vec["g_ln"].reshape(16, 128).T
        pc[:, 16:32] = g_final.reshape(16, 128).T
        sch = shift_chunk_cols(hh)
        for s in range(13):
            pc[:, 32 + s] = vec["mu_shift"][sch[s]]
        for j in range(4):
            pcj = 64 * (8 * hh + 2 * j) + np.arange(128)
            for q, nm in enumerate(("w0", "a0", "k_k", "k_a")):
                pc[:, 58 + 7 * j + q] = vec[nm][pcj]
            pc[:, 58 + 7 * j + 4] = r_k[pcj]
            pc[:, 58 + 7 * j + 5] = vec["gn_a_g"][pcj]
            pc[:, 58 + 7 * j + 6] = vec["gn_a_b"][pcj]
        for hb in range(2):
            H = 2 * hh + hb
            for vc in range(2):
                cols = 256 * H + 128 * vc + np.arange(128)
                pc[:, 86 + 2 * (2 * hb + vc)] = vec["gn_b_g"][cols]
                pc[:, 86 + 2 * (2 * hb + vc) + 1] = vec["gn_b_b"][cols]
        d["pc"] = pc
        d["cst"] = consts(hh)
        d["rot"] = rot
        tsl = slice(TH * hh, TH * hh + TH)
        ssl = slice(32 * b + NSH * hh, 32 * b + NSH * hh + NSH)
        d["xres"] = f(np.concatenate([xp[b, tsl].T, xs[ssl].T], axis=1))
        d["pT"] = f(np.concatenate([pp[b, tsl].T, ps[ssl].T], axis=1))
        d["shs"] = f(np.stack([sshift[sb][:, sch[s]].T for s in range(13)]))
        st = swkv[sb][:, 8 * hh:8 * hh + 8]
        st = st.reshape(32, 4, 2, 64, 64).transpose(1, 2, 4, 0, 3)
        d["wkv_in"] = f(st.reshape(4, 128, 32, 64))
        perm = np.concatenate([2 * np.arange(128), 2 * np.arange(128) + 1])
        rs = sret[sb][:, 2 * hh:2 * hh + 2][:, :, perm, :]
        d["ret_in"] = f(rs.reshape(32, 2, 2, 128, 256).transpose(1, 0, 2, 3, 4))
        sel = np.zeros((128, 2), np.float32); sel[:, hh] = 1.0
        d["sel"] = sel
        per_core.append(d)
    return per_core


def assemble(results):
    y_p = np.zeros((4, T, D), np.float32); y_s = np.zeros((128, 1, D), np.float32)
    wkv_p = np.zeros((1, 4, 16, 64, 64), np.float32); shift_p = np.zeros((1, 4, A_SHIFT), np.float32)
    ret_p = np.zeros((1, 4, 4, 256, 256), np.float32)
    wkv_s = np.zeros((1, 128, 16, 64, 64), np.float32); shift_s = np.zeros((1, 128, A_SHIFT), np.float32)
    ret_s = np.zeros((1, 128, 4, 256, 256), np.float32)
    perm = np.concatenate([2 * np.arange(128), 2 * np.arange(128) + 1])
    for c in range(8):
        b, hh = c // 2, c % 2
        r = dict(results[c])
        for nm, shp in (("yT", (D, NTH)), ("shp", (128, 16)), ("shs_o", (13, 128, 32)), ("wkv_p", (4, 128, 64)),
                        ("wkv_s", (4, 128, 32, 64)), ("ret_p", (2, 2, 128, 256)), ("ret_s", (2, 32, 2, 128, 256))):
            r[nm] = np.asarray(r[nm]).reshape(shp)
        yT = r["yT"]
        y_p[b, TH * hh:TH * hh + TH] = yT[:, :TH].T
        y_s[32 * b + NSH * hh:32 * b + NSH * hh + NSH, 0] = yT[:, TH:].T
        sch = shift_chunk_cols(hh)
        for s in range(13):
            shift_p[0, b, sch[s]] = r["shp"][:, s]
            shift_s[0, 32 * b:32 * b + 32][:, sch[s]] = r["shs_o"][s].T
        wp = r["wkv_p"].reshape(4, 2, 64, 64)
        wkv_p[0, b, 8 * hh:8 * hh + 8] = wp.transpose(0, 1, 3, 2).reshape(8, 64, 64)
        ws = r["wkv_s"].reshape(4, 2, 64, 32, 64)
        wkv_s[0, 32 * b:32 * b + 32, 8 * hh:8 * hh + 8] = ws.transpose(3, 0, 1, 4, 2).reshape(32, 8, 64, 64)
        rp = r["ret_p"].reshape(2, 256, 256)
        ret_p[0, b, 2 * hh:2 * hh + 2][:, perm, :] = rp
        rs = r["ret_s"].transpose(1, 0, 2, 3, 4).reshape(32, 2, 256, 256)
        ret_s[0, 32 * b:32 * b + 32, 2 * hh:2 * hh + 2][:, :, perm, :] = rs
    return (y_p, y_s, wkv_p, shift_p, ret_p, wkv_s, shift_s, ret_s)


KAPPA = float(np.exp(-0.5))
RMS_EPS = 1e-6
GN_EPS_A = 64 * 1e-5
GN_EPS_B = 1e-5
ARENA_BYTES = 212480


class Ctx:
    def send_view(self, r0, t0, n):
        i = min(t0 // TH, 2)
        o = t0 - i * TH
        return self.send[i][r0:r0 + 128, o:o + n]


def dram_in(nc, name, shape, dt=F32):
    return Tile(nc.dram_tensor(name, list(shape), dt, kind="ExternalInput").ap(), name)


def dram_out(nc, name, shape, dt=F32):
    return Tile(nc.dram_tensor(name, list(shape), dt, kind="ExternalOutput").ap(), name)


def setup(nc, es, stage, dbg_shape=None):
    C = Ctx()
    C.nc = nc
    C.S = Sched(nc)
    S = C.S
    I = C.I = {}
    I["xTp"] = dram_in(nc, "xTp", [8, 128, 16 * 256])
    I["xTs"] = dram_in(nc, "xTs", [128, 16 * NS])
    I["w1"] = dram_in(nc, "w1", [33, 128, 2048])
    I["w2"] = dram_in(nc, "w2", [32, 128, 2048])
    I["w3"] = dram_in(nc, "w3", [16, 128, 256])
    I["wB"] = dram_in(nc, "wB", [128, 512])
    I["pc"] = dram_in(nc, "pc", [128, NPC])
    I["cst"] = dram_in(nc, "cst", [128, NCST])
    I["rot"] = dram_in(nc, "rot", [128, 2, NT])
    I["xres"] = dram_in(nc, "xres", [D, NTH])
    I["pT"] = dram_in(nc, "pT", [256, NTH])
    I["shs"] = dram_in(nc, "shs", [13, 128, 32])
    I["wkv_in"] = dram_in(nc, "wkv_in", [4, 128, 32, 64])
    I["ret_in"] = dram_in(nc, "ret_in", [2, 32, 2, 128, 256])
    I["sel"] = dram_in(nc, "sel", [128, 2])
    O = C.O = {}
    O["yT"] = dram_out(nc, "yT", [D, NTH])
    O["shp"] = dram_out(nc, "shp", [128, 16])
    O["shs_o"] = dram_out(nc, "shs_o", [13, 128, 32])
    O["wkv_p"] = dram_out(nc, "wkv_p", [4, 128, 64])
    O["wkv_s"] = dram_out(nc, "wkv_s", [4, 128, 32, 64])
    O["ret_p"] = dram_out(nc, "ret_p", [2, 2, 128, 256])
    O["ret_s"] = dram_out(nc, "ret_s", [2, 32, 2, 128, 256])
    if dbg_shape is not None:
        O["dbg"] = dram_out(nc, "dbg", dbg_shape)
    C.send = [Tile(nc.dram_tensor(f"send{i}", [1024, w], BF16).ap(), f"send{i}") for i, w in enumerate((TH, TH, NS))]
    C.recv = [Tile(nc.dram_tensor(f"recv{i}", [2048, w], BF16).ap(), f"recv{i}") for i, w in enumerate((TH, TH, NS))]
    arena_t = es.enter_context(nc.sbuf_tensor("arena", [128, ARENA_BYTES], U8))
    C.A = Arena(arena_t[:, :], ARENA_BYTES, "sb")
    C.ps = [Tile(es.enter_context(nc.psum_tensor(f"ps{i}", [128, 512], F32))[:, :], f"ps{i}") for i in range(8)]
    A = C.A
    C.cst = A.tile([128, NCST], F32, "cst")
    C.pc = A.tile([128, NPC], F32, "pc")
    C.omu = A.tile([128, 13], F32, "omu")
    C.identbf = A.tile([128, 128], BF16, "identbf")
    C.onesbf = A.tile([128, 128], BF16, "onesbf")
    C.bonesbf = A.tile([128, 128], BF16, "bonesbf")
    C.bones64 = A.tile([128, 128], F32, "bones64")
    C.ones256 = A.tile([128, 128], F32, "ones256")
    C.sel = A.tile([128, 2], F32, "sel")
    S.dma("sp", C.cst, I["cst"], stream="c")
    S.dma("sp", C.pc, I["pc"], stream="c")
    S.dma("sp", C.sel, I["sel"], stream="c")
    S.copy(C.identbf, C.cst[:, C_ID:C_ID + 128], eng="dve")
    S.memset(C.onesbf, 1.0, eng="pool")
    S.copy(C.bonesbf, C.cst[:, C_BONES:C_BONES + 128], eng="dve")
    S.ts(C.bones64, C.cst[:, C_BONES:C_BONES + 128], 1.0 / 64.0, None, ALU.mult, eng="dve")
    S.memset(C.ones256, 1.0 / 256.0, eng="pool")
    S.ts(C.omu, C.pc[:, 32:45], -1.0, 1.0, ALU.mult, ALU.add, eng="dve")
    C.ident = C.cst[:, C_ID:C_ID + 128]
    return C


def phase0_rmsnorm(C):
    S, A, I = C.S, C.A, C.I
    C.uT = A.tile([128, 16, NT], BF16, "uT")
    A.mark()
    TW = 256
    NB = 4
    xb = [A.tile([128, 16, TW], F32, f"xb{i}") for i in range(NB)]
    sq = [A.tile([128, 16, TW], BF16, f"sq{i}") for i in range(NB)]
    rt = [A.tile([128, TW], F32, f"rt{i}") for i in range(NB)]
    tiles = [(t0, TW) for t0 in range(0, T, TW)] + [(T, NS)]
    for it, (t0, n) in enumerate(tiles):
        b = it % NB
        src = I["xTp"][it].re("p (kc t) -> p kc t", kc=16) if it < 8 else I["xTs"].re("p (kc t) -> p kc t", kc=16)
        S.dma("sp", xb[b][:, :, 0:n], src, stream=f"x{b}")
        S.act(sq[b][:, :, 0:n], xb[b][:, :, 0:n], AF.Square)
        ps = C.ps[it % 2]
        for kc in range(16):
            S.mm(ps[:, 0:n], C.onesbf, sq[b][:, kc, 0:n], start=(kc == 0), stop=(kc == 15))
        S.act(rt[b][:, 0:n], ps[:, 0:n], AF.Sqrt, bias=RMS_EPS, scale=1.0 / D)
        S.recip(rt[b][:, 0:n], rt[b][:, 0:n])
        for kc in range(16):
            S.stt(C.uT[:, kc, t0:t0 + n], xb[b][:, kc, 0:n], C.pc[:, kc:kc + 1], rt[b][:, 0:n], ALU.mult, ALU.mult)
    A.release()


MSTOP = os.environ.get("MSTOP")

NEG = ALU.mult


def phase1_alloc(C):
    A = C.A
    C.wf = [A.tile([128, 16, 128], F32, f"wf{i}") for i in range(2)]
    C.wbf = [A.tile([128, 16, 128], BF16, f"wbf{i}") for i in range(4)]
    C.wld = 0
    C.wBf = A.tile([128, 512], F32, "wBf")
    C.wB = A.tile([128, 512], BF16, "wB")
    C.wa_act = A.tile([128, NT], BF16, "wa_act")
    C.shp_sb = A.tile([128, 16], F32, "shp_sb")
    S = C.S
    S.dma("sp", C.wBf, C.I["wB"], stream="c")
    S.copy(C.wB, C.wBf, eng="pool")


def load_w(C, ch, slot, src="w1"):
    S = C.S
    b = C.wld % len(C.wf)
    C.wld += 1
    S.dma("sp", C.wf[b], C.I[src][ch].re("p (kc n) -> p kc n", kc=16), stream=f"w{b}")
    S.copy(C.wbf[slot][:, 0:8, :], C.wf[b][:, 0:8, :], eng="act")
    S.copy(C.wbf[slot][:, 8:16, :], C.wf[b][:, 8:16, :], eng="dve")


def inproj(C, ps, slot, t0, n):
    S = C.S
    for kc in range(16):
        S.mm(ps[:, 0:n], C.wbf[slot][:, kc, :], C.uT[:, kc, t0:t0 + n], start=(kc == 0), stop=(kc == 15))


def shifted(C, ps, zst, s, tt, n, xs, tmp, sample_prev=None):
    S = C.S
    mu = C.pc[:, 32 + s:33 + s]
    omu = C.omu[:, s:s + 1]
    if sample_prev is None:
        if tt == 0:
            S.memset(zst[:, 0:1], 0.0, eng="pool")
        else:
            S.copy(zst[:, 0:1], zst[:, 512:513], eng="act")
        S.copy(zst[:, 1:1 + n], ps[:, 0:n], eng="act")
        S.act(tmp[:, 0:n], zst[:, 0:n], AF.Copy, scale=mu)
        S.stt(xs[:, 0:n], zst[:, 1:1 + n], omu, tmp[:, 0:n], ALU.mult, ALU.add)
        if tt == 3:
            S.copy(C.shp_sb[:, s:s + 1], zst[:, 512:513], eng="act")
    else:
        zs = sample_prev["z"]
        S.copy(zs[:, 0:n], ps[:, 0:n], eng="act")
        S.dma("pool", C.O["shs_o"][s], zs[:, 0:n], stream="po")
        S.act(tmp[:, 0:n], sample_prev["prev"][:, 0:n], AF.Copy, scale=mu)
        S.stt(xs[:, 0:n], zs[:, 0:n], omu, tmp[:, 0:n], ALU.mult, ALU.add)


def phase1a_wa(C):
    S, A = C.S, C.A
    A.mark()
    zst = A.tile([128, 513], F32, "zst_wa")
    xs = A.tile([128, 512], F32, "xs_wa")
    tmp = A.tile([128, 512], F32, "tmp_wa")
    zs = A.tile([128, 32], F32, "zs_wa")
    prev = A.tile([128, 32], F32, "prev_wa")
    load_w(C, 0, 0)
    S.dma("sp", prev, C.I["shs"][0], stream="c")
    for tt in range(5):
        t0, n = (tt * 512, 512) if tt < 4 else (T, NS)
        ps = C.ps[tt % 2]
        inproj(C, ps, 0, t0, n)
        shifted(C, ps, zst, 0, tt, n, xs, tmp, None if tt < 4 else {"z": zs, "prev": prev})
        S.act(C.wa_act[0:64, t0:t0 + n], xs[0:64, 0:n], AF.Tanh)
        S.copy(C.wa_act[64:128, t0:t0 + n], xs[64:128, 0:n], eng="dve")
    A.release()


def rwkv_alloc(C):
    A = C.A
    R = Ctx2()
    f = lambda name, w=512: A.tile([128, w], F32, name)
    R.zst = [A.tile([128, 513], F32, f"zst{i}") for i in range(3)]
    R.t = {k: f(k) for k in ("sg", "a", "cp", "d", "eC", "eNC", "eNCp", "eCC", "xk", "kkn", "kp", "bv", "xr", "xv", "t1", "t2")}
    R.kk2 = A.tile([128, 512], BF16, "kk2")
    R.ARt = [A.tile([128, 4, 256], BF16, f"ARt{i}") for i in range(2)]
    R.BtT = [A.tile([128, 512], BF16, f"BtT{i}") for i in range(2)]
    R.KtT = [A.tile([128, 512], BF16, f"KtT{i}") for i in range(2)]
    R.Bh = A.tile([128, 512], BF16, "Bh")
    R.Kh = A.tile([128, 512], BF16, "Kh")
    R.Vf = A.tile([128, 512], BF16, "Vf")
    R.Btm = A.tile([128, 4, 128], BF16, "Btm")
    R.Ktm = A.tile([128, 4, 128], BF16, "Ktm")
    R.Vtm = A.tile([128, 4, 128], BF16, "Vtm")
    R.bonus = [A.tile([128, 512], F32, f"bonus{i}") for i in range(2)]
    R.gate = [A.tile([128, 512], BF16, f"gate{i}") for i in range(2)]
    R.wC = [A.tile([128, 4], F32, f"wC{i}") for i in range(2)]
    R.QA = [A.tile([128, 4, 256], BF16, f"QA{h}") for h in range(2)]
    R.KA = [A.tile([128, 4, 256], BF16, f"KA{h}") for h in range(2)]
    R.P = [[A.tile([128, 4, 128], BF16, f"P{b}_{h}") for h in range(2)] for b in range(2)]
    R.Q = [[A.tile([128, 4, 128], BF16, f"Q{b}_{h}") for h in range(2)] for b in range(2)]
    R.Xb = [[A.tile([128, 4, 128], BF16, f"Xb{b}_{h}") for h in range(2)] for b in range(2)]
    R.X7 = [A.tile([128, 4, 128], BF16, f"X7_{h}") for h in range(2)]
    R.McT = [A.tile([128, 128], F32, f"McT{c}") for c in range(4)]
    R.Sbd = [A.tile([128, 128], BF16, f"Sbd{i}") for i in range(2)]
    R.QpT = [A.tile([128, 128], BF16, f"QpT{c}") for c in range(4)]
    R.Sf = [A.tile([128, 64], F32, f"Sf{i}") for i in range(2)]
    R.yT = A.tile([128, 512], F32, "yT")
    R.mixo = A.tile([128, 512], BF16, "mixo")
    R.s = {k: A.tile([128, 32], F32, "s_" + k) for k in ("z0", "z1", "z2", "p0", "p1", "p2", "w", "na", "bv", "kp", "r", "v")}
    v3 = lambda k: R.t[k][:, :].re("p (n v) -> p n v", n=8)
    R.ST, R.st1, R.st2, R.Sn = v3("eC"), v3("eCC"), v3("cp"), v3("d")
    v4 = lambda k: R.t[k][:, :].re("p (n v) -> p n v", n=4)
    R.abd = [v4("eNC"), v4("eNCp")]
    R.vbd = [v4("xk"), v4("kkn")]
    R.Snbd = [v4("xr"), v4("xv")]
    return R


class Ctx2:
    pass


def rwkv_prep_tile(C, R, j, tt, par=0):
    S = C.S
    t = R.t
    t0, n = (tt * 512, 512) if tt < 4 else (T, NS)
    pcj = lambda q: C.pc[:, 58 + 7 * j + q:59 + 7 * j + q]
    psA, psB, psM = C.ps[0], C.ps[0], C.ps[1]
    ARt, BtT, KtT, wC, bonus, gate = R.ARt[par], R.BtT[par], R.KtT[par], R.wC[par], R.bonus[par], R.gate[par]
    S.mm(psM[:, 0:n], C.wB[0:64, 128 * j:128 * j + 128], C.wa_act[0:64, t0:t0 + n])
    S.act(t["sg"][:, 0:n], psM[:, 0:n], AF.Sigmoid, bias=pcj(0))
    S.mm(psM[:, 0:n], C.wB[64:128, 128 * j:128 * j + 128], C.wa_act[64:128, t0:t0 + n])
    S.act(t["a"][:, 0:n], psM[:, 0:n], AF.Sigmoid, bias=pcj(1))
    yield
    samp = None
    if tt < 4:
        S.op("dve", lambda e, o=t["cp"].ap, m=C.cst.ap[:, C_RESET:C_RESET + 512], s=t["sg"].ap:
             e.tensor_tensor_scan(out=o, data0=m, data1=s, initial=0.0, op0=ALU.mult, op1=ALU.add),
             ins=(C.cst, t["sg"]), outs=(t["cp"],))
        S.act(t["eC"], t["cp"], AF.Exp, scale=KAPPA)
        S.act(t["eNC"], t["cp"], AF.Exp, scale=-KAPPA)
        S.tt(t["d"], t["cp"], t["sg"], ALU.subtract)
        S.act(t["eNCp"], t["d"], AF.Exp, scale=-KAPPA)
        cend = t["cp"][:, :].re("p (c t) -> p c t", t=128)[:, :, 127:128]
        S.tt(t["d"][:, :].re("p (c t) -> p c t", t=128), t["cp"][:, :].re("p (c t) -> p c t", t=128),
             cend.bc([128, 4, 128]), ALU.subtract)
        S.act(t["eCC"], t["d"], AF.Exp, scale=KAPPA)
        S.act(wC[:, :].un(2), cend, AF.Exp, scale=-KAPPA)
    yield
    inproj(C, psA, 0, t0, n)
    if tt == 4:
        samp = {"z": R.s["z0"], "prev": R.s["p0"]}
    shifted(C, psA, R.zst[0], 1 + 3 * j, tt, n, t["xk"], t["t1"], samp)
    S.act(R.kk2[:, 0:n], t["xk"][:, 0:n], AF.Square, scale=pcj(2))
    yield
    inproj(C, psB, 1, t0, n)
    if tt == 4:
        samp = {"z": R.s["z1"], "prev": R.s["p1"]}
    shifted(C, psB, R.zst[1], 2 + 3 * j, tt, n, t["xr"], t["t1"], samp)
    S.mm(psM[:, 0:n], C.bonesbf, R.kk2[:, 0:n])
    yield
    S.act(t["t1"][:, 0:n], psM[:, 0:n], AF.Sqrt)
    S.ts(t["t1"][:, 0:n], t["t1"][:, 0:n], 1e-12, None, ALU.max)
    S.recip(t["t1"][:, 0:n], t["t1"][:, 0:n])
    S.stt(t["kkn"][:, 0:n], t["xk"][:, 0:n], pcj(2), t["t1"][:, 0:n], ALU.mult, ALU.mult)
    S.ts(t["t2"][:, 0:n], t["a"][:, 0:n], -1.0, pcj(3), ALU.add, ALU.mult)
    S.stt(t["kp"][:, 0:n], t["t2"][:, 0:n], 1.0, t["xk"][:, 0:n], ALU.add, ALU.mult)
    S.tt(t["bv"][:, 0:n], t["kkn"][:, 0:n], t["a"][:, 0:n], ALU.mult)
    yield
    if tt < 4:
        r3 = lambda v: v[:, :].re("p (c t) -> p c t", t=128)
        S.stt(ARt[:, :, 0:128], r3(t["kkn"]), -1.0, r3(t["eNCp"]), ALU.mult, ALU.mult)
        S.tt(BtT, t["bv"], t["eC"], ALU.mult)
        S.tt(R.Bh, t["bv"], t["eCC"], ALU.mult)
        S.tt(KtT, t["kp"], t["eC"], ALU.mult)
        S.tt(R.Kh, t["kp"], t["eCC"], ALU.mult)
    else:
        S.act(R.s["w"], t["sg"][:, 0:n], AF.Exp, scale=-KAPPA)
        S.ts(R.s["na"], t["kkn"][:, 0:n], -1.0, None, ALU.mult)
        S.copy(R.s["bv"], t["bv"][:, 0:n])
        S.copy(R.s["kp"], t["kp"][:, 0:n])
    if tt < 4:
        S.tt(ARt[:, :, 128:256], r3(t["xr"]), r3(t["eNC"]), ALU.mult)
    else:
        S.copy(R.s["r"], t["xr"][:, 0:n])
    S.stt(R.kk2[:, 0:n], t["xr"][:, 0:n], pcj(4), t["kp"][:, 0:n], ALU.mult, ALU.mult)
    yield
    inproj(C, psA, 2, t0, n)
    S.mm(psM[:, 0:n], C.bonesbf, R.kk2[:, 0:n])
    if tt == 4:
        samp = {"z": R.s["z2"], "prev": R.s["p2"]}
    shifted(C, psA, R.zst[2], 3 + 3 * j, tt, n, t["xv"], t["t1"], samp)
    S.tt(bonus[:, 0:n], psM[:, 0:n], t["xv"][:, 0:n], ALU.mult)
    if tt < 4:
        S.copy(R.Vf, t["xv"], eng="act")
    else:
        S.copy(R.s["v"], t["xv"][:, 0:n])
    yield
    inproj(C, psB, 3, t0, n)
    S.act(gate[:, 0:n], psB[:, 0:n], AF.Silu)
    yield


def rwkv_transposes(C, R):
    S = C.S
    for src, dst, pb in ((R.Bh, R.Btm, C.ps[3]), (R.Kh, R.Ktm, C.ps[4]), (R.Vf, R.Vtm, C.ps[5])):
        for c in range(4):
            S.mm(pb[:, 128 * c:128 * c + 128], src[:, 128 * c:128 * c + 128], C.identbf)
        S.copy(dst[:, :, :], pb[:, :].re("p (c f) -> p c f", c=4), eng="act")


def rwkv_machinery(C, R, first_batch, par=0):
    S = C.S
    ps = C.ps
    M1 = C.cst[:, C_M1:C_M1 + 256]
    MSL = C.cst[:, C_MSL:C_MSL + 128]
    hs = lambda h: slice(64 * h, 64 * h + 64)
    ARt, BtT_, KtT_, wC = R.ARt[par], R.BtT[par], R.KtT[par], R.wC[par]
    v4 = lambda bank: bank[:, :].re("p (c f) -> p c f", c=4)
    for (src, dst) in ((BtT_, R.QA), (KtT_, R.KA)):
        for h in range(2):
            for cl in range(4):
                pb = ps[4 + 2 * h + cl // 2]
                S.mm(pb[:, 256 * (cl % 2):256 * (cl % 2) + 256], src[hs(h), 128 * cl:128 * cl + 128], ARt[hs(h), cl, :])
        for h in range(2):
            for half in range(2):
                pb = ps[4 + 2 * h + half]
                S.tt(dst[h][:, 2 * half:2 * half + 2, :], pb[:, :].re("p (c f) -> p c f", c=2),
                     M1.un(1).bc([128, 2, 256]), ALU.mult)
        yield
    for h in range(2):
        for cl in range(4):
            S.mm(ps[4 + h][:, 128 * cl:128 * cl + 128], ARt[hs(h), cl, 0:128], BtT_[hs(h), 128 * cl:128 * cl + 128])
    for h in range(2):
        S.tt(R.P[0][h], v4(ps[4 + h]), MSL.un(1).bc([128, 4, 128]), ALU.mult)
    yield
    for h in range(2):
        pb = ps[2 + h]
        for cl in range(4):
            o = 128 * cl
            S.mm(pb[:, o:o + 64], ARt[hs(h), cl, 0:128], C.identbf[hs(h), hs(h)], start=(cl == 0), stop=False,
                 skip_group_check=True)
            S.mm(pb[:, o + 64:o + 128], R.KA[h][:, cl, 0:128], R.Vtm[:, cl, hs(h)], start=False, stop=False,
                 skip_group_check=True)
    for h in range(2):
        S.copy(R.Xb[0][h], v4(ps[2 + h]), eng=("act" if h else "dve"))
    yield
    for k in range(7):
        cur, nxt = k % 2, (k + 1) % 2
        Qk = (lambda h, cl: R.QA[h][:, cl, 0:128]) if k == 0 else (lambda h, cl, cur=cur: R.Q[cur][h][:, cl, :])
        for h in range(2):
            for cl in range(4):
                o = 128 * cl
                if k < 5:
                    S.mm(ps[4 + h][:, o:o + 128], Qk(h, cl), R.P[cur][h][:, cl, :])
                S.mm(ps[2 + h][:, o:o + 128], Qk(h, cl), R.Xb[cur][h][:, cl, :], start=False, stop=(k == 6),
                     skip_group_check=True)
                if k < 6:
                    S.mm(ps[6 + h][:, o:o + 128], R.P[cur][h][:, cl, :], Qk(h, cl))
        yield
        for h in range(2):
            if k < 5:
                S.copy(R.P[nxt][h], v4(ps[4 + h]), eng="act")
            if k < 6:
                S.copy(R.Q[nxt][h], v4(ps[6 + h]), eng=("act" if h else "dve"))
                S.copy(R.Xb[nxt][h], v4(ps[2 + h]), eng=("act" if h else "dve"))
            else:
                S.copy(R.X7[h], v4(ps[2 + h]), eng=("act" if h else "dve"))
        yield
    for h in range(2):
        for cl in range(4):
            S.mm(ps[4][hs(h), 64 * cl:64 * cl + 64], R.X7[h][:, cl, 0:64], R.Btm[:, cl, hs(h)])
            S.mm(ps[5][hs(h), 128 * cl:128 * cl + 128], R.X7[h][:, cl, 0:64], R.QA[h][:, cl, 128:256])
    for cl in range(4):
        if first_batch:
            S.memset(R.McT[cl], 0.0, eng="pool")
        for h in range(2):
            S.stt(R.McT[cl][hs(h), hs(h)], C.cst[hs(h), C_ID64:C_ID64 + 64], wC[hs(h), cl:cl + 1],
                  ps[4][hs(h), 64 * cl:64 * cl + 64], ALU.mult, ALU.add)
        S.tt(R.QpT[cl], ps[5][:, 128 * cl:128 * cl + 128], ARt[:, cl, 128:256], ALU.add)
    yield
    if first_batch:
        S.memset(R.Sf[0], 0.0, eng="pool")
        for i in range(2):
            S.memset(R.Sbd[i], 0.0, eng="pool")
        R.si = 0
    for cl in range(4):
        si, so = R.si, 1 - R.si
        yo = ps[6][:, 128 * cl:128 * cl + 128]
        so_ = ps[7][:, 64 * cl:64 * cl + 64]
        for h in range(2):
            S.mm(yo[hs(h), :], R.X7[h][:, cl, 64:128], R.QA[h][:, cl, 128:256], start=True, stop=False)
            S.mm(yo[hs(h), :], R.Vtm[:, cl, hs(h)], R.KA[h][:, cl, 128:256], start=False, stop=False)
            S.mm(so_[hs(h), :], R.Btm[:, cl, hs(h)], R.X7[h][:, cl, 64:128], start=True, stop=False)
            S.mm(so_[hs(h), :], R.Ktm[:, cl, hs(h)], R.Vtm[:, cl, hs(h)], start=False, stop=False)
        S.mm(yo, R.Sbd[si], R.QpT[cl], start=False, stop=True)
        S.mm(so_, R.McT[cl], R.Sf[si], start=False, stop=True)
        S.copy(R.Sf[so], so_, eng="dve")
        for h in range(2):
            S.copy(R.Sbd[so][hs(h), hs(h)], so_[hs(h), :], eng="dve")
        R.si = so
        yield
    S.copy(R.yT, ps[6], eng="act")
    yield


def groupnorm_a(C, R, j, n, yT, t0, par=0):
    S = C.S
    t = R.t
    pcj = lambda q: C.pc[:, 58 + 7 * j + q:59 + 7 * j + q]
    pm, pq = C.ps[0], C.ps[1]
    S.act(t["t1"][:, 0:n], yT[:, 0:n], AF.Square)
    S.mm(pm[:, 0:n], C.bones64, yT[:, 0:n])
    S.mm(pq[:, 0:n], C.bones64, t["t1"][:, 0:n])
    S.copy(t["t2"][:, 0:n], pm[:, 0:n], eng="act")
    S.tt(t["t1"][:, 0:n], t["t2"][:, 0:n], t["t2"][:, 0:n], ALU.mult)
    S.tt(t["t1"][:, 0:n], pq[:, 0:n], t["t1"][:, 0:n], ALU.subtract)
    S.act(t["t1"][:, 0:n], t["t1"][:, 0:n], AF.Sqrt, bias=GN_EPS_A)
    S.recip(t["t1"][:, 0:n], t["t1"][:, 0:n])
    S.tt(t["t2"][:, 0:n], yT[:, 0:n], t["t2"][:, 0:n], ALU.subtract)
    S.tt(t["t2"][:, 0:n], t["t2"][:, 0:n], t["t1"][:, 0:n], ALU.mult)
    S.act(t["t2"][:, 0:n], t["t2"][:, 0:n], AF.Identity, bias=pcj(6), scale=pcj(5))
    S.tt(t["t2"][:, 0:n], t["t2"][:, 0:n], R.bonus[par][:, 0:n], ALU.add)
    S.tt(R.mixo[:, 0:n], t["t2"][:, 0:n], R.gate[par][:, 0:n], ALU.mult)
    S.dma("pool", C.send_view(128 * j, t0, n), R.mixo[:, 0:n], stream="pm")


def rwkv_sample(C, R, j):
    S = C.S
    ps = C.ps
    hs = lambda h: slice(64 * h, 64 * h + 64)
    id64 = C.cst[:, C_ID64:C_ID64 + 64]
    for i in range(2):
        S.memset(R.abd[i], 0.0, eng="pool")
        S.memset(R.vbd[i], 0.0, eng="pool")
        S.memset(R.Snbd[i], 0.0, eng="pool")
    STb = [R.ST, R.t["sg"][:, :].re("p (n v) -> p n v", n=8)]

    def stage_a(g):
        n0 = 8 * g
        ST = STb[g % 2]
        S.dma("sp", ST, C.I["wkv_in"][j][:, n0:n0 + 8, :], stream=f"st{g % 2}")
        for half in range(2):
            m0 = n0 + 4 * half
            for h in range(2):
                S.copy(R.abd[half][hs(h), :, hs(h)], R.s["na"][hs(h), m0:m0 + 4].un(2).bc([64, 4, 64]))
                S.copy(R.vbd[half][hs(h), :, hs(h)], R.s["v"][hs(h), m0:m0 + 4].un(2).bc([64, 4, 64]))
            pb = ps[2 + 2 * (g % 2) + half]
            for nl in range(4):
                o = 128 * nl
                S.mm(pb[:, o:o + 64], R.abd[half][:, nl, :], ST[:, 4 * half + nl, :])
                S.mm(pb[:, o + 64:o + 128], R.vbd[half][:, nl, :], id64)

    def stage_b(g):
        n0 = 8 * g
        ST = STb[g % 2]
        bc = lambda k: R.s[k][:, n0:n0 + 8].un(2).bc([128, 8, 64])
        S.tt(R.st1, ST, bc("w"), ALU.mult)
        for half in range(2):
            sl = slice(4 * half, 4 * half + 4)
            pv = ps[2 + 2 * (g % 2) + half][:, :].re("p (n f) -> p n f", n=4)
            bch = lambda k: R.s[k][:, n0 + 4 * half:n0 + 4 * half + 4].un(2).bc([128, 4, 64])
            S.tt(R.st2[:, sl, :], pv[:, :, 0:64], bch("bv"), ALU.mult)
            S.tt(R.Sn[:, sl, :], pv[:, :, 64:128], bch("kp"), ALU.mult)
        S.tt(R.st1, R.st1, R.st2, ALU.add)
        S.tt(R.Sn, R.Sn, R.st1, ALU.add)
        S.dma("pool", C.O["wkv_s"][j][:, n0:n0 + 8, :], R.Sn, stream="po")
        for half in range(2):
            for h in range(2):
                S.copy(R.Snbd[half][hs(h), :, hs(h)], R.Sn[hs(h), 4 * half:4 * half + 4, :])
            for nl in range(4):
                n = n0 + 4 * half + nl
                S.mm(ps[0][:, n:n + 1], R.Snbd[half][:, nl, :], R.s["r"][:, n:n + 1])

    stage_a(0)
    for g in range(4):
        if g + 1 < 4:
            stage_a(g + 1)
        stage_b(g)
    S.copy(R.yT[:, 0:32], ps[0][:, 0:32], eng="act")


def _drain(g):
    for _ in g:
        pass


def _interleave(gm, gp):
    STOP = object()
    am = ap = True
    while am or ap:
        if am:
            for _ in range(2):
                if next(gm, STOP) is STOP:
                    am = False
                    break
        if ap:
            if gp is None or next(gp, STOP) is STOP:
                ap = False


def phase1b_rwkv(C, pairs=range(4), stop_after=None):
    S, A = C.S, C.A
    A.mark()
    R = rwkv_alloc(C)
    C.R = R
    pairs = list(pairs)

    def load_pair(j):
        for q in range(4):
            load_w(C, 1 + 4 * j + q, q)
        for q in range(3):
            S.dma("sp", R.s[f"p{q}"], C.I["shs"][1 + 3 * j + q], stream=f"sh{q}")
    if pairs:
        load_pair(pairs[0])
    for ip, j in enumerate(pairs):
        _drain(rwkv_prep_tile(C, R, j, 0, 0))
        rwkv_transposes(C, R)
        for tt in range(4):
            gm = rwkv_machinery(C, R, tt == 0, tt % 2)
            gp = rwkv_prep_tile(C, R, j, tt + 1, (tt + 1) % 2) if tt < 3 else None
            _interleave(gm, gp)
            if tt < 3:
                rwkv_transposes(C, R)
            groupnorm_a(C, R, j, 512, R.yT, tt * 512, tt % 2)
        S.dma("pool", C.O["wkv_p"][j], R.Sf[R.si], stream="po")
        _drain(rwkv_prep_tile(C, R, j, 4, 0))
        if ip + 1 < len(pairs):
            load_pair(pairs[ip + 1])
        rwkv_sample(C, R, j)
        groupnorm_a(C, R, j, 32, R.yT, T, 0)
    A.release()
    return R


def groupnorm_b(C, B, hb, n, t0):
    S = C.S
    pm, pq = C.ps[6], C.ps[7]
    S.act(B.sq[:, :, 0:n], B.yB[:, :, 0:n], AF.Square)
    for vc in range(2):
        S.mm(pm[:, 0:n], C.ones256, B.yB[:, vc, 0:n], start=(vc == 0), stop=(vc == 1))
    for vc in range(2):
        S.mm(pq[:, 0:n], C.ones256, B.sq[:, vc, 0:n], start=(vc == 0), stop=(vc == 1))
    S.copy(B.g1[:, 0:n], pm[:, 0:n], eng="act")
    S.tt(B.g2[:, 0:n], B.g1[:, 0:n], B.g1[:, 0:n], ALU.mult)
    S.tt(B.g2[:, 0:n], pq[:, 0:n], B.g2[:, 0:n], ALU.subtract)
    S.act(B.g2[:, 0:n], B.g2[:, 0:n], AF.Sqrt, bias=GN_EPS_B)
    S.recip(B.g2[:, 0:n], B.g2[:, 0:n])
    for vc in range(2):
        q = 86 + 2 * (2 * hb + vc)
        S.tt(B.g3[:, 0:n], B.yB[:, vc, 0:n], B.g1[:, 0:n], ALU.subtract)
        S.tt(B.g3[:, 0:n], B.g3[:, 0:n], B.g2[:, 0:n], ALU.mult)
        S.ts(B.g3[:, 0:n], B.g3[:, 0:n], C.pc[:, q:q + 1], C.pc[:, q + 1:q + 2], ALU.mult, ALU.add)
        S.tt(B.mixo[:, 0:n], B.g3[:, 0:n], B.gate[:, vc, t0:t0 + n], ALU.mult)
        r0 = 512 + 256 * hb + 128 * vc
        S.dma("pool", C.send_view(r0, t0, n), B.mixo[:, 0:n], stream="pm")


def ret_alloc(C):
    A = C.A
    B = Ctx2()
    B.cs = A.tile([128, 2, 512], F32, "cs")
    B.qrT = A.tile([128, 2, T], BF16, "qrT")
    B.krT = A.tile([128, 2, T], BF16, "krT")
    B.gate = A.tile([128, 2, NT], BF16, "gateB")
    B.Vtm = A.tile([128, 16, 256], BF16, "VtmB")
    B.Ktm = A.tile([128, 16, 256], BF16, "KtmB")
    B.S0b = A.tile([128, 4, 2, 256], F32, "S0b")
    flat = B.S0b.ap.rearrange("p n x v -> p (n x v)")
    B.xa = Tile(flat[:, 0:512], "xa")
    B.xb = Tile(flat[:, 512:1024], "xb")
    B.t = [Tile(flat[:, 1024:1536], "rt0"), Tile(flat[:, 1536:2048], "rt1"), A.tile([128, 512], F32, "rt2"), A.tile([128, 512], F32, "rt3")]
    B.yB = A.tile([128, 2, 512], F32, "yB")
    B.sq = A.tile([128, 2, 512], F32, "sqB")
    B.g1 = A.tile([128, 512], F32, "g1")
    B.g2 = A.tile([128, 512], F32, "g2")
    B.g3 = A.tile([128, 512], F32, "g3")
    B.mixo = A.tile([128, 512], BF16, "mixoB")
    B.Sf = [A.tile([128, 2, 256], F32, f"SfB{i}") for i in range(2)]
    B.Sb = [A.tile([128, 2, 256], BF16, f"SbB{i}") for i in range(2)]
    B.scT = [A.tile([128, 128], BF16, f"scT{i}") for i in range(2)]
    B.qs = A.tile([128, 2, 32], F32, "q_s")
    B.ks = A.tile([128, 2, 32], F32, "k_s")
    B.vs = A.tile([128, 2, 32], F32, "v_s")
    B.S0 = A.tile([128, 4, 2, 256], F32, "S0")
    B.vbc = A.tile([128, 2, 4, 128], F32, "vbcB")
    return B


def rotary(C, B, ps_a, ps_b, hb, tt, n, is_q, dstT, dst_s):
    S = C.S
    t0 = tt * 512 if tt < 4 else T
    cos, sin = B.cs[:, 0, 0:n], B.cs[:, 1, 0:n]
    if is_q and tt < 4:
        gq = C.cst[:, C_GQ + 512 * hb:C_GQ + 512 * hb + 512]
        S.tt(B.xa[:, 0:n], ps_a[:, 0:n], gq[:, 0:n], ALU.mult)
        S.tt(B.xb[:, 0:n], ps_b[:, 0:n], gq[:, 0:n], ALU.mult)
    elif is_q:
        S.copy(B.xa[:, 0:n], ps_a[:, 0:n], eng="act")
        S.copy(B.xb[:, 0:n], ps_b[:, 0:n], eng="act")
    else:
        S.act(B.xa[:, 0:n], ps_a[:, 0:n], AF.Copy, scale=1.0 / 16.0)
        S.act(B.xb[:, 0:n], ps_b[:, 0:n], AF.Copy, scale=1.0 / 16.0)
    t = B.t
    S.tt(t[0][:, 0:n], B.xa[:, 0:n], cos, ALU.mult)
    S.tt(t[1][:, 0:n], B.xb[:, 0:n], sin, ALU.mult)
    S.tt(t[2][:, 0:n], B.xa[:, 0:n], sin, ALU.mult)
    S.tt(t[3][:, 0:n], B.xb[:, 0:n], cos, ALU.mult)
    if tt < 4:
        S.tt(dstT[:, 0, t0:t0 + n], t[0][:, 0:n], t[1][:, 0:n], ALU.subtract)
        S.tt(dstT[:, 1, t0:t0 + n], t[2][:, 0:n], t[3][:, 0:n], ALU.add)
    else:
        S.tt(dst_s[:, 0, :], t[0][:, 0:n], t[1][:, 0:n], ALU.subtract)
        S.tt(dst_s[:, 1, :], t[2][:, 0:n], t[3][:, 0:n], ALU.add)


def ret_head(C, B, hb):
    S = C.S
    ps = C.ps
    base = 17 + 8 * hb
    ident = C.cst[:, C_ID:C_ID + 128]
    for q in range(4):
        load_w(C, base + q, q)
    for tt in range(5):
        t0, n = (tt * 512, 512) if tt < 4 else (T, NS)
        S.dma("sp", B.cs[:, :, 0:n], C.I["rot"][:, :, t0:t0 + n], stream="r")
        inproj(C, ps[0], 0, t0, n)
        inproj(C, ps[1], 1, t0, n)
        rotary(C, B, ps[0], ps[1], hb, tt, n, True, B.qrT, B.qs)
        inproj(C, ps[2], 2, t0, n)
        inproj(C, ps[3], 3, t0, n)
        rotary(C, B, ps[2], ps[3], hb, tt, n, False, B.krT, B.ks)
    for q in range(4):
        load_w(C, base + 4 + q, q)
    for tt in range(5):
        t0, n = (tt * 512, 512) if tt < 4 else (T, NS)
        for vc in range(2):
            inproj(C, ps[vc], vc, t0, n)
            S.act(B.gate[:, vc, t0:t0 + n], ps[vc][:, 0:n], AF.Silu)
    for vc in range(2):
        inproj(C, ps[2 + vc], 2 + vc, T, NS)
        S.copy(B.vs[:, vc, :], ps[2 + vc][:, 0:NS], eng="act")
    for c in range(16):
        pb = ps[c % 2]
        for vc in range(2):
            for kc in range(16):
                S.mm(pb[:, 128 * vc:128 * vc + 128], C.uT[:, kc, 128 * c:128 * c + 128], C.wbf[2 + vc][:, kc, :],
                     start=(kc == 0), stop=(kc == 15), skip_group_check=True)
        S.copy(B.Vtm[:, c, :], pb[:, 0:256], eng=("act" if c % 2 else "dve"))
    for c in range(16):
        pb = ps[2 + c % 2]
        for X in range(2):
            S.mm(pb[:, 128 * X:128 * X + 128], B.krT[:, X, 128 * c:128 * c + 128], C.identbf)
        S.ts(B.Ktm[:, c, :], pb[:, 0:256], C.cst[:, C_KDEC + hb:C_KDEC + hb + 1], None, ALU.mult)
    gam = C.cst[:, C_GAM + hb:C_GAM + hb + 1]
    S0bufs = [B.S0, B.S0b]
    def sample_load(g):
        S.dma("sp", S0bufs[g % 2][:, :, :, :].re("p n x v -> p (n x) v"),
              C.I["ret_in"][hb, 4 * g:4 * g + 4].re("n x p v -> p (n x) v"), stream=f"s{g % 2}")

    def sample_group(g):
        n0 = 4 * g
        B_S0 = S0bufs[g % 2]
        if g + 1 < 8:
            sample_load(g + 1)
        for vc in range(2):
            S.copy(B.vbc[:, vc, :, :], B.vs[:, vc, n0:n0 + 4].un(2).bc([128, 4, 128]))
        S0f = B_S0[:, :, :, :].re("p n x v -> p (n x v)")
        S.ts(S0f, S0f, gam, None, ALU.mult)
        def vbc_mm(nl):
            for vc in range(2):
                S.mm(ps[2 + nl % 2][:, 128 * vc:128 * vc + 128], B.vbc[:, vc, nl, :], ident)
        vbc_mm(0)
        for nl in range(4):
            n = n0 + nl
            pv = ps[2 + nl % 2]
            if nl + 1 < 4:
                vbc_mm(nl + 1)
            for X in range(2):
                S.stt(B_S0[:, nl, X, :], pv[:, 0:256], B.ks[:, X, n:n + 1], B_S0[:, nl, X, :], ALU.mult, ALU.add)
            for vc in range(2):
                for X in range(2):
                    S.mm(ps[4 + vc][:, n:n + 1], B_S0[:, nl, X, 128 * vc:128 * vc + 128], B.qs[:, X, n:n + 1],
                         start=(X == 0), stop=(X == 1), skip_group_check=True)
        S.dma("sp", C.O["ret_s"][hb, n0:n0 + 4].re("n x p v -> p (n x) v"), B_S0[:, :, :, :].re("p n x v -> p (n x) v"), stream=f"so{g % 2}")
    S.barrier()
    sample_load(0)
    S.memset(B.Sf[0], 0.0, eng="pool")
    S.memset(B.Sb[0], 0.0, eng="pool")
    si = 0
    DT = C.cst[:, C_DT + 128 * hb:C_DT + 128 * hb + 128]
    g128 = C.cst[:, C_G128 + hb:C_G128 + hb + 1]
    for c in range(16):
        so = 1 - si
        cs_ = slice(128 * c, 128 * c + 128)
        sc = B.scT[c % 2]
        pS_, pO_ = ps[0], ps[1]
        for X in range(2):
            S.mm(pS_[:, 0:128], B.krT[:, X, cs_], B.qrT[:, X, cs_], start=(X == 0), stop=(X == 1))
        S.tt(sc, pS_[:, 0:128], DT, ALU.mult)
        for X in range(2):
            S.mm(ps[2 + X][:, 0:256], B.Ktm[:, c, 128 * X:128 * X + 128], B.Vtm[:, c, :])
        for vc in range(2):
            po = pO_[:, 128 * vc:128 * vc + 128]
            S.mm(po, B.Vtm[:, c, 128 * vc:128 * vc + 128], sc, start=True, stop=False, skip_group_check=True)
            for X in range(2):
                S.mm(po, B.Sb[si][:, X, 128 * vc:128 * vc + 128], B.qrT[:, X, cs_], start=False, stop=(X == 1),
                     skip_group_check=True)
        S.copy(B.yB[:, :, 128 * (c % 4):128 * (c % 4) + 128], pO_[:, 0:256].re("p (v t) -> p v t", v=2), eng="act")
        for X in range(2):
            S.stt(B.Sf[so][:, X, :], B.Sf[si][:, X, :], g128, ps[2 + X][:, 0:256], ALU.mult, ALU.add)
        S.copy(B.Sb[so], B.Sf[so], eng="act")
        si = so
        if c % 4 == 3:
            groupnorm_b(C, B, hb, 512, 128 * (c - 3))
            if hb == 1 and c == 7:
                exchange(C, (0,))
        if c % 2 == 1:
            sample_group(c // 2)
    S.dma("pool", C.O["ret_p"][hb].re("x p v -> p x v"), B.Sf[si], stream="po")
    for vc in range(2):
        S.copy(B.yB[:, vc, 0:NS], ps[4 + vc][:, 0:NS], eng="act")
    groupnorm_b(C, B, hb, NS, T)
    S.barrier()


def phase1c_retention(C):
    S, A = C.S, C.A
    A.mark()
    B = ret_alloc(C)
    for hb in range(2):
        ret_head(C, B, hb)
    A.release()


def exchange(C, which=(0, 1, 2)):
    S = C.S
    for i in which:
        sb, rb = C.send[i].ap, C.recv[i].ap
        S.custom("pool", lambda e, sb=sb, rb=rb: e.collective_compute(
            "AllGather", ALU.bypass, replica_groups=[[0, 1], [2, 3], [4, 5], [6, 7]], ins=[sb.opt()], outs=[rb.opt()]),
            f"cc{i}", 1, ins=(C.send[i],), outs=(C.recv[i],))


def phase2(C):
    S, A = C.S, C.A
    ps = C.ps
    A.mark()
    C.wf = [A.tile([128, 16, 128], F32, f"wf2_{i}") for i in range(3)]
    C.wbf = [A.tile([128, 16, 128], BF16, f"wbf2_{i}") for i in range(3)]
    h1 = A.tile([128, 16, NTH], F32, "h1")
    h1b = A.tile([128, 16, NTH], BF16, "h1b")
    A.mark()
    mixsel_l = [A.tile([128, NTH], BF16, f"mixsel{kc}") for kc in range(16)]
    cand = [[A.tile([128, TH], BF16, f"cand{i}_{b}") for b in range(2)] for i in range(2)]
    cands = [A.tile([128, NS], BF16, f"cands{i}") for i in range(2)]
    tmpb = [A.tile([128, TH], F32, f"tmpb{i}") for i in range(2)]
    xr = [A.tile([128, NTH], F32, f"xr{i}") for i in range(2)]
    s0, s1 = C.sel[:, 0:1], C.sel[:, 1:2]
    for kc in range(16):
        b = kc % 2
        rows = slice(128 * kc, 128 * kc + 128)
        S.dma("sp", cand[b][0], C.recv[0][rows, :], stream=f"ca{b}")
        S.dma("sp", cand[b][1], C.recv[1][rows, :], stream=f"cb{b}")
        S.dma("sp", cands[b], C.recv[2][rows, 0:NS], stream=f"cs{b}")
        S.ts(tmpb[b], cand[b][0], s0, None, ALU.mult)
        S.stt(mixsel_l[kc][:, 0:TH], cand[b][1], s1, tmpb[b], ALU.mult, ALU.add)
        S.ts(tmpb[b][:, 0:NSH], cands[b][:, 0:NSH], s0, None, ALU.mult)
        S.stt(mixsel_l[kc][:, TH:NTH], cands[b][:, NSH:NS], s1, tmpb[b][:, 0:NSH], ALU.mult, ALU.add)
    tiles = [(0, 512), (512, 512), (1024, NSH)]
    load_w(C, 0, 0, src="w2")
    for dc in range(16):
        if dc + 1 < 16:
            load_w(C, dc + 1, (dc + 1) % 3, src="w2")
        S.dma("sp", xr[dc % 2], C.I["xres"][128 * dc:128 * dc + 128, :], stream=f"xr{dc % 2}")
        for it, (t0, n) in enumerate(tiles):
            pb = ps[(3 * dc + it) % 4]
            for kc in range(16):
                S.mm(pb[:, 0:n], C.wbf[dc % 3][:, kc, :], mixsel_l[kc][:, t0:t0 + n], start=(kc == 0), stop=(kc == 15))
            S.tt(h1[:, dc, t0:t0 + n], pb[:, 0:n], xr[dc % 2][:, t0:t0 + n], ALU.add)
            S.copy(h1b[:, dc, t0:t0 + n], h1[:, dc, t0:t0 + n], eng="act")
    S.barrier()
    A.release()
    pTf = A.tile([128, 2, NTH], F32, "pTf")
    pTb = A.tile([128, 2, NTH], BF16, "pTb")
    wpf = [A.tile([128, 2, 128], F32, f"wpf{i}") for i in range(2)]
    wpb = [A.tile([128, 2, 128], BF16, f"wpb{i}") for i in range(3)]
    sig = [A.tile([128, 512], F32, f"sig{i}") for i in range(2)]
    sq = [A.tile([128, 512], BF16, f"sq2_{i}") for i in range(2)]
    rstd = A.tile([128, NTH], F32, "rstd")
    yo = [A.tile([128, NTH], F32, f"yo{i}") for i in range(4)]
    S.dma("sp", pTf, C.I["pT"].re("(kc p) t -> p kc t", p=128), stream="c")
    S.copy(pTb, pTf, eng="act")
    ssq = [ps[5], ps[6], ps[7]]
    def load_gate(dc):
        load_w(C, 16 + dc, dc % 3, src="w2")
        S.dma("sp", wpf[dc % 2], C.I["w3"][dc].re("p (kc n) -> p kc n", kc=2), stream=f"wp{dc % 2}")
        S.copy(wpb[dc % 3], wpf[dc % 2], eng="act")
    pend = []

    def flush_ssq():
        while pend:
            it_, k_, n_, dc_ = pend.pop(0)
            S.mm(ssq[it_][:, 0:n_], C.onesbf, sq[k_][:, 0:n_], start=(dc_ == 0), stop=(dc_ == 15), skip_group_check=True)
    load_gate(0)
    for dc in range(16):
        if dc + 1 < 16:
            load_gate(dc + 1)
        for it, (t0, n) in enumerate(tiles):
            k = (3 * dc + it) % 2
            pg, pp = ps[k], ps[2 + k]
            for kc in range(16):
                S.mm(pg[:, 0:n], C.wbf[dc % 3][:, kc, :], h1b[:, kc, t0:t0 + n], start=(kc == 0), stop=(kc == 15))
            for kc in range(2):
                S.mm(pp[:, 0:n], wpb[dc % 3][:, kc, :], pTb[:, kc, t0:t0 + n], start=(kc == 0), stop=(kc == 1))
            flush_ssq()
            S.act(sig[k][:, 0:n], pg[:, 0:n], AF.Sigmoid)
            S.tt(sig[k][:, 0:n], sig[k][:, 0:n], pp[:, 0:n], ALU.mult)
            S.tt(h1[:, dc, t0:t0 + n], h1[:, dc, t0:t0 + n], sig[k][:, 0:n], ALU.add)
            S.act(sq[k][:, 0:n], h1[:, dc, t0:t0 + n], AF.Square)
            pend.append((it, k, n, dc))
    flush_ssq()
    for it, (t0, n) in enumerate(tiles):
        S.act(rstd[:, t0:t0 + n], ssq[it][:, 0:n], AF.Sqrt, bias=RMS_EPS, scale=1.0 / D)
        S.recip(rstd[:, t0:t0 + n], rstd[:, t0:t0 + n])
    for dc in range(16):
        y = yo[dc % 4]
        for it, (t0, n) in enumerate(tiles):
            S.stt(y[:, t0:t0 + n], h1[:, dc, t0:t0 + n], C.pc[:, 16 + dc:17 + dc], rstd[:, t0:t0 + n], ALU.mult, ALU.mult)
        S.dma("sp", C.O["yT"][128 * dc:128 * dc + 128, :], y, stream=f"y{dc % 4}")
    A.release()


def build(pairs=range(4), do_ret=True, do_p2=True):
    nc = bass.Bass("TRN2", target_bir_lowering=False)
    with ExitStack() as es:
        C = setup(nc, es, 0)
        S, A = C.S, C.A
        A.mark()
        phase0_rmsnorm(C)
        S.barrier()
        phase1_alloc(C)
        phase1a_wa(C)
        S.barrier()
        phase1b_rwkv(C, pairs=pairs)
        S.dma("pool", C.O["shp"], C.shp_sb, stream="po")
        S.barrier()
        if do_ret:
            phase1c_retention(C)
        if do_p2:
            exchange(C, (1, 2))
            S.barrier()
            A.release()
            if os.environ.get("P2ONLYX"):
                t = A.tile([128, 2080], BF16, "xchk")
                S.dma("sp", t[:, 0:1024], C.recv[0][1024:1152, :], stream="c")
                t2 = A.tile([128, 512], F32, "xchk2")
                S.copy(t2, t[:, 0:512])
                S.dma("sp", C.O["yT"][0:128, 0:512], t2, stream="c")
            else:
                phase2(C)
        print("ops", S.nops, {e: len(S.q[e]) for e in ENGS}, flush=True)
        S.emit()
    return nc


_NC = None


def kernel(**inputs):
    global _NC
    per_core = prep_inputs(inputs)
    if _NC is None:
        _NC = build()
    res = run_bass_kernel_spmd(_NC, per_core, core_ids=list(range(8)))
    return assemble(res.results)
```

```python
import os
from contextlib import ExitStack
import numpy as np
import concourse.bass as bass
import concourse.mybir as mybir
from concourse.bass_utils import run_bass_kernel_spmd


F32 = mybir.dt.float32
BF16 = mybir.dt.bfloat16
U8 = mybir.dt.uint8
AF = mybir.ActivationFunctionType
ALU = mybir.AluOpType
DT_SIZE = {F32: 4, BF16: 2, U8: 1}

ENGS = ("pe", "act", "dve", "pool", "sp")


class Tile:
    __slots__ = ("ap", "name", "writers", "readers")

    def __init__(self, ap, name):
        self.ap = ap
        self.name = name
        self.writers = {}
        self.readers = {}

    def __getitem__(self, k):
        return V(self, self.ap[k])

    def re(self, s, **kw):
        return V(self, self.ap.rearrange(s, **kw))


class V:
    __slots__ = ("t", "ap")

    def __init__(self, t, ap):
        self.t = t
        self.ap = ap

    def __getitem__(self, k):
        return V(self.t, self.ap[k])

    def re(self, s, **kw):
        return V(self.t, self.ap.rearrange(s, **kw))

    def bc(self, shape):
        return V(self.t, self.ap.to_broadcast(list(shape)))

    def un(self, axis):
        return V(self.t, self.ap.unsqueeze(axis))


def _ap(x):
    return x.ap if isinstance(x, (V, Tile)) else x


def _tl(x):
    if isinstance(x, V):
        return x.t
    if isinstance(x, Tile):
        return x
    return None


class Sched:
    def __init__(self, nc):
        self.nc = nc
        self.q = {e: [] for e in ENGS}
        self.cnt = {e: 0 for e in ENGS}
        self.dcnt = {}
        self.seen = {e: {} for e in ENGS}
        self.nops = 0

    def _deps(self, eng, reads, writes):
        own = "c_" + eng
        n = self.cnt[eng]
        need = {}

        def add(k, v):
            if k == own:
                if eng in ("pe", "sp"):
                    return
            if need.get(k, 0) < v:
                need[k] = v

        for t in reads:
            for k, v in t.writers.items():
                add(k, v)
        for t in writes:
            for k, v in t.writers.items():
                add(k, v)
            for k, v in t.readers.items():
                add(k, v)
        waits = []
        seen = self.seen[eng]
        for k, v in need.items():
            if seen.get(k, 0) >= v:
                continue
            seen[k] = v
            waits.append((k, v))
        return waits

    def _mark(self, tok, reads, writes):
        k, v = tok
        for t in writes:
            t.writers = {k: v}
            t.readers = {}
        for t in reads:
            if t in writes:
                continue
            if t.readers.get(k, 0) < v:
                t.readers[k] = v

    def op(self, eng, fn, ins=(), outs=()):
        reads = [t for t in (_tl(x) for x in ins) if t is not None]
        writes = [t for t in (_tl(x) for x in outs) if t is not None]
        waits = self._deps(eng, reads, writes)
        self.cnt[eng] += 1
        tok = ("c_" + eng, self.cnt[eng])
        self.q[eng].append((waits, fn, tok[0], 1))
        self._mark(tok, reads, writes)
        self.nops += 1

    def dma(self, eng, out, in_, stream="d0", **kw):
        reads = [t for t in [_tl(in_)] if t is not None]
        writes = [t for t in [_tl(out)] if t is not None]
        waits = self._deps(eng, reads, writes)
        key = "d_" + stream
        prev = self.dcnt.get(key, 0)
        if prev > 0 and self.seen[eng].get(key, 0) < prev:
            self.seen[eng][key] = prev
            waits = [w for w in waits if w[0] != key] + [(key, prev)]
        self.dcnt[key] = prev + 16
        tok = (key, self.dcnt[key])
        o, i = _ap(out), _ap(in_)
        self.q[eng].append((waits, lambda e: e.dma_start(out=o, in_=i, **kw), key, 16))
        self._mark(tok, reads, writes)
        self.nops += 1
        return tok

    def custom(self, eng, fn, key, inc, ins=(), outs=()):
        reads = [t for t in (_tl(x) for x in ins) if t is not None]
        writes = [t for t in (_tl(x) for x in outs) if t is not None]
        waits = self._deps(eng, reads, writes)
        self.dcnt[key] = self.dcnt.get(key, 0) + inc
        tok = (key, self.dcnt[key])
        self.q[eng].append((waits, fn, key, inc))
        self._mark(tok, reads, writes)

    def barrier(self):
        allk = {("c_" + e): self.cnt[e] for e in ENGS if self.cnt[e] > 0}
        allk.update(self.dcnt)
        for e in ENGS:
            waits = []
            for k, v in allk.items():
                if k == "c_" + e and e in ("pe", "sp"):
                    continue
                if k != "c_" + e and self.seen[e].get(k, 0) >= v:
                    continue
                self.seen[e][k] = v
                waits.append((k, v))
            if waits:
                self.q[e].append((waits, None, None, 0))

    def mm(self, out, lhsT, rhs, start=True, stop=True, **kw):
        o, l, r = _ap(out), _ap(lhsT), _ap(rhs)
        self.op("pe", lambda e: e.matmul(o, l, r, start=start, stop=stop, **kw),
                ins=(lhsT, rhs), outs=(out,))

    def act(self, out, in_, func, bias=0.0, scale=1.0, eng="act", extra_ins=()):
        o, i = _ap(out), _ap(in_)
        b, s = _ap(bias), _ap(scale)
        self.op(eng, lambda e: e.activation(out=o, in_=i, func=func, bias=b, scale=s),
                ins=(in_, bias, scale) + tuple(extra_ins), outs=(out,))

    def tt(self, out, in0, in1, op, eng="dve"):
        o, a, b = _ap(out), _ap(in0), _ap(in1)
        self.op(eng, lambda e: e.tensor_tensor(out=o, in0=a, in1=b, op=op), ins=(in0, in1), outs=(out,))

    def ts(self, out, in0, s1, s2, op0, op1=None, eng="dve"):
        o, a = _ap(out), _ap(in0)
        x1, x2 = _ap(s1), _ap(s2)
        if op1 is None:
            self.op(eng, lambda e: e.tensor_scalar(out=o, in0=a, scalar1=x1, scalar2=None, op0=op0),
                    ins=(in0, s1), outs=(out,))
        else:
            self.op(eng, lambda e: e.tensor_scalar(out=o, in0=a, scalar1=x1, scalar2=x2, op0=op0, op1=op1),
                    ins=(in0, s1, s2), outs=(out,))

    def stt(self, out, in0, scalar, in1, op0, op1, eng="dve"):
        o, a, b = _ap(out), _ap(in0), _ap(in1)
        s = _ap(scalar)
        self.op(eng, lambda e: e.scalar_tensor_tensor(out=o, in0=a, scalar=s, in1=b, op0=op0, op1=op1),
                ins=(in0, scalar, in1), outs=(out,))

    def copy(self, out, in_, eng="dve"):
        o, i = _ap(out), _ap(in_)
        if eng == "act":
            self.op(eng, lambda e: e.activation(out=o, in_=i, func=AF.Copy), ins=(in_,), outs=(out,))
        else:
            self.op(eng, lambda e: e.tensor_copy(out=o, in_=i), ins=(in_,), outs=(out,))

    def recip(self, out, in_):
        o, i = _ap(out), _ap(in_)
        self.op("dve", lambda e: e.reciprocal(out=o, in_=i), ins=(in_,), outs=(out,))

    def memset(self, out, val, eng="pool"):
        o = _ap(out)
        self.op(eng, lambda e: e.memset(o, val), outs=(out,))

    def emit(self):
        nc = self.nc
        self.barrier()
        keys = set(self.dcnt.keys()) | {"c_" + e for e in ENGS}
        with ExitStack() as es:
            sems = {k: es.enter_context(nc.semaphore(k)) for k in sorted(keys)}
            block = es.enter_context(nc.Block())

            def replay(name):
                def f(e):
                    for waits, fn, key, inc in self.q[name]:
                        for k, v in waits:
                            e.wait_ge(sems[k], v)
                        if fn is None:
                            continue
                        ins = fn(e)
                        ins.then_inc(sems[key], inc)
                return f

            block.tensor(replay("pe"))
            block.scalar(replay("act"))
            block.vector(replay("dve"))
            block.gpsimd(replay("pool"))
            block.sync(replay("sp"))


class Arena:
    def __init__(self, ap, nbytes, name):
        self.base = ap
        self.n = nbytes
        self.off = 0
        self.name = name
        self.marks = []

    def tile(self, shape, dtype, name):
        sz = DT_SIZE[dtype]
        free = int(np.prod(shape[1:])) * sz
        self.off = (self.off + 31) // 32 * 32
        assert self.off + free <= self.n, f"arena {self.name} overflow at {name}: {self.off}+{free}>{self.n}"
        v = self.base[:, self.off:self.off + free].bitcast(dtype)
        self.off += free
        if len(shape) > 2:
            names = " ".join(f"a{i}" for i in range(len(shape) - 1))
            kw = {f"a{i}": shape[i + 1] for i in range(len(shape) - 1)}
            v = v.rearrange(f"p ({names}) -> p {names}", **kw)
        if shape[0] < 128:
            v = v[0:shape[0]]
        return Tile(v, name)

    def mark(self):
        self.marks.append(self.off)

    def release(self):
        self.off = self.marks.pop()


D = 2048; T = 2048; NS = 32; NT = T + NS; TH = 1024; NSH = 16; NTH = TH + NSH
D_A = 1024; D_B = 1024; HD_B = 256
A_SHIFT = 3 * D_A + 128
OFF_GA = A_SHIFT
OFF_B = A_SHIFT + D_A
NPC = 94
POS_S = 16384.0

C_ID = 0; C_M1 = 128; C_MSL = 384; C_BONES = 512; C_ID64 = 640; C_RESET = 704
C_DT = 1216; C_GQ = 1472; C_KDEC = 2496; C_G128 = 2498; C_GAM = 2500; NCST = 2502


def col_chunks(hh):
    ch = []
    ch.append(3 * D_A + np.arange(128))
    for j in range(4):
        pc = 64 * (8 * hh + 2 * j) + np.arange(128)
        ch.append(D_A + pc)
        ch.append(0 * D_A + pc)
        ch.append(2 * D_A + pc)
        ch.append(OFF_GA + pc)
    for hb in range(2):
        H = 2 * hh + hb
        ev = 256 * H + 2 * np.arange(128)
        od = ev + 1
        ch.append(OFF_B + ev)
        ch.append(OFF_B + od)
        ch.append(OFF_B + D_B + ev)
        ch.append(OFF_B + D_B + od)
        ch.append(OFF_B + 3 * D_B + 256 * H + np.arange(128))
        ch.append(OFF_B + 3 * D_B + 256 * H + 128 + np.arange(128))
        ch.append(OFF_B + 2 * D_B + 256 * H + np.arange(128))
        ch.append(OFF_B + 2 * D_B + 256 * H + 128 + np.arange(128))
    return ch


def shift_chunk_cols(hh):
    ch = col_chunks(hh)
    out = [ch[0]]
    for j in range(4):
        out += [ch[1 + 4 * j], ch[2 + 4 * j], ch[3 + 4 * j]]
    return out


def mix_rows(hh):
    rows = []
    for j in range(4):
        rows.append(64 * (8 * hh + 2 * j) + np.arange(128))
    for hb in range(2):
        H = 2 * hh + hb
        for vc in range(2):
            rows.append(D_A + 256 * H + 128 * vc + np.arange(128))
    return np.concatenate(rows)


def tile_w(w, rows_chunked=16):
    K, n = w.shape
    kc = K // 128
    t = w.reshape(kc, 128, n // 128, 128)
    return np.ascontiguousarray(t.transpose(2, 1, 0, 3)).reshape(n // 128, 128, kc * 128)


def consts(hh):
    c = np.zeros((128, NCST), np.float32)
    p = np.arange(128)[:, None]
    f = np.arange(128)[None, :]
    c[:, C_ID:C_ID + 128] = (p == f)
    c[:, C_M1:C_M1 + 128] = (p < f)
    c[:, C_M1 + 128:C_M1 + 256] = (p <= f)
    c[:, C_MSL:C_MSL + 128] = (f < p)
    c[:, C_BONES:C_BONES + 128] = ((p // 64) == (f // 64))
    c[:, C_ID64:C_ID64 + 64] = ((p % 64) == np.arange(64)[None, :])
    c[:, C_RESET:C_RESET + 512] = ((np.arange(512) % 128) != 0)[None, :]
    for hb in range(2):
        H = 2 * hh + hb
        gam = np.float64(1.0) - np.exp2(np.float64(-5.0 - H))
        gam = np.float64(np.float32(gam))
        lg = np.log(gam)
        c[:, C_DT + 128 * hb:C_DT + 128 * hb + 128] = np.where(p <= f, np.exp(-(p + 1.0) * lg), 0.0)
        c[:, C_GQ + 512 * hb:C_GQ + 512 * hb + 512] = np.exp(((np.arange(512) % 128) + 1.0) * lg)[None, :]
        c[:, C_KDEC + hb] = np.exp((127.0 - np.arange(128)) * lg)
        c[:, C_G128 + hb] = np.exp(128.0 * lg)
        c[:, C_GAM + hb] = gam
    return c


def rot_table():
    ang = (1.0 / (np.float32(10000.0) ** np.linspace(0.0, 1.0, 128, dtype=np.float32))).astype(np.float32)
    pos = np.concatenate([np.arange(T, dtype=np.float32), np.full(NS, POS_S, np.float32)])
    th = (pos[None, :] * ang[:, None]).astype(np.float32)
    th64 = th.astype(np.float64)
    r = np.zeros((128, 2, NT), np.float32)
    r[:, 0] = np.cos(th64)
    r[:, 1] = np.sin(th64)
    return r


def prep_inputs(inp):
    f = lambda a: np.ascontiguousarray(a, dtype=np.float32)
    w_in = np.asarray(inp["w_in"])[0]
    w_out = np.asarray(inp["w_out"])[0]
    w_gate = np.asarray(inp["w_ple_gate"])[0]
    w_ple = np.asarray(inp["w_ple"])[0]
    xp = np.asarray(inp["x_prompt"]); xs = np.asarray(inp["x_sample"])[:, 0]
    pp = np.asarray(inp["p_prompt"])[0]; ps = np.asarray(inp["p_sample"])[0][:, 0]
    swkv = np.asarray(inp["state_wkv"])[0]; sshift = np.asarray(inp["state_shift"])[0]
    sret = np.asarray(inp["state_ret"])[0]
    rot = rot_table()
    vec = {k: np.asarray(inp[k])[0] for k in ("g_ln", "mu_shift", "w0", "a0", "k_k", "k_a", "gn_a_g", "gn_a_b", "gn_b_g", "gn_b_b")}
    r_k = np.asarray(inp["r_k"])[0].reshape(-1)
    w_wB = np.asarray(inp["w_wB"])[0]; w_aB = np.asarray(inp["w_aB"])[0]
    g_final = np.asarray(inp["g_final"])
    mrows = np.concatenate([mix_rows(0), mix_rows(1)])
    w2 = np.concatenate([tile_w(w_out[mrows, :]), tile_w(w_gate)], axis=0)
    w3 = tile_w(w_ple)
    per_core = []
    for c in range(8):
        b, hh = c // 2, c % 2
        d = {}
        sb = slice(32 * b, 32 * b + 32)
        xpt = xp[b].T.reshape(16, 128, 8, 256).transpose(2, 1, 0, 3)
        d["xTp"] = f(xpt.reshape(8, 128, 16 * 256))
        d["xTs"] = f(xs[sb].T.reshape(16, 128, NS).transpose(1, 0, 2).reshape(128, 16 * NS))
        chs = col_chunks(hh)
        d["w1"] = f(tile_w(w_in[:, np.concatenate(chs)]))
        d["w2"] = f(w2)
        d["w3"] = f(w3)
        acols = 64 * 8 * hh + np.arange(512)
        d["wB"] = f(np.concatenate([w_wB[:, acols], w_aB[:, acols]], axis=0))
        pc = np.zeros((128, NPC), np.float32)
        pc[:, 0:16] = vec["g_ln"].reshape(16, 128).T
        pc[:, 16:32] = g_final.reshape(16, 128).T
        sch = shift_chunk_cols(hh)
        for s in range(13):
            pc[:, 32 + s] = vec["mu_shift"][sch[s]]
        for j in range(4):
            pcj = 64 * (8 * hh + 2 * j) + np.arange(128)
            for q, nm in enumerate(("w0", "a0", "k_k", "k_a")):
                pc[:, 58 + 7 * j + q] = vec[nm][pcj]
            pc[:, 58 + 7 * j + 4] = r_k[pcj]
            pc[:, 58 + 7 * j + 5] = vec["gn_a_g"][pcj]
            pc[:, 58 + 7 * j + 6] = vec["gn_a_b"][pcj]
        for hb in range(2):
            H = 2 * hh + hb
            for vc in range(2):
                cols = 256 * H + 128 * vc + np.arange(128)
                pc[:, 86 + 2 * (2 * hb + vc)] = vec["gn_b_g"][cols]
                pc[:, 86 + 2 * (2 * hb + vc) + 1] = vec["gn_b_b"][cols]
        d["pc"] = pc
        d["cst"] = consts(hh)
        d["rot"] = rot
        tsl = slice(TH * hh, TH * hh + TH)
        ssl = slice(32 * b + NSH * hh, 32 * b + NSH * hh + NSH)
        d["xres"] = f(np.concatenate([xp[b, tsl].T, xs[ssl].T], axis=1))
        d["pT"] = f(np.concatenate([pp[b, tsl].T, ps[ssl].T], axis=1))
        d["shs"] = f(np.stack([sshift[sb][:, sch[s]].T for s in range(13)]))
        st = swkv[sb][:, 8 * hh:8 * hh + 8]
        st = st.reshape(32, 4, 2, 64, 64).transpose(1, 2, 4, 0, 3)
        d["wkv_in"] = f(st.reshape(4, 128, 32, 64))
        perm = np.concatenate([2 * np.arange(128), 2 * np.arange(128) + 1])
        rs = sret[sb][:, 2 * hh:2 * hh + 2][:, :, perm, :]
        d["ret_in"] = f(rs.reshape(32, 2, 2, 128, 256).transpose(1, 0, 2, 3, 4))
        sel = np.zeros((128, 2), np.float32); sel[:, hh] = 1.0
        d["sel"] = sel
        per_core.append(d)
    return per_core


def assemble(results):
    y_p = np.zeros((4, T, D), np.float32); y_s = np.zeros((128, 1, D), np.float32)
    wkv_p = np.zeros((1, 4, 16, 64, 64), np.float32); shift_p = np.zeros((1, 4, A_SHIFT), np.float32)
    ret_p = np.zeros((1, 4, 4, 256, 256), np.float32)
    wkv_s = np.zeros((1, 128, 16, 64, 64), np.float32); shift_s = np.zeros((1, 128, A_SHIFT), np.float32)
    ret_s = np.zeros((1, 128, 4, 256, 256), np.float32)
    perm = np.concatenate([2 * np.arange(128), 2 * np.arange(128) + 1])
    for c in range(8):
        b, hh = c // 2, c % 2
        r = dict(results[c])
        for nm, shp in (("yT", (D, NTH)), ("shp", (128, 16)), ("shs_o", (13, 128, 32)), ("wkv_p", (4, 128, 64)),
                        ("wkv_s", (4, 128, 32, 64)), ("ret_p", (2, 2, 128, 256)), ("ret_s", (2, 32, 2, 128, 256))):
            r[nm] = np.asarray(r[nm]).reshape(shp)
        yT = r["yT"]
        y_p[b, TH * hh:TH * hh + TH] = yT[:, :TH].T
        y_s[32 * b + NSH * hh:32 * b + NSH * hh + NSH, 0] = yT[:, TH:].T
        sch = shift_chunk_cols(hh)
        for s in range(13):
            shift_p[0, b, sch[s]] = r["shp"][:, s]
            shift_s[0, 32 * b:32 * b + 32][:, sch[s]] = r["shs_o"][s].T
        wp = r["wkv_p"].reshape(4, 2, 64, 64)
        wkv_p[0, b, 8 * hh:8 * hh + 8] = wp.transpose(0, 1, 3, 2).reshape(8, 64, 64)
        ws = r["wkv_s"].reshape(4, 2, 64, 32, 64)
        wkv_s[0, 32 * b:32 * b + 32, 8 * hh:8 * hh + 8] = ws.transpose(3, 0, 1, 4, 2).reshape(32, 8, 64, 64)
        rp = r["ret_p"].reshape(2, 256, 256)
        ret_p[0, b, 2 * hh:2 * hh + 2][:, perm, :] = rp
        rs = r["ret_s"].transpose(1, 0, 2, 3, 4).reshape(32, 2, 256, 256)
        ret_s[0, 32 * b:32 * b + 32, 2 * hh:2 * hh + 2][:, :, perm, :] = rs
    return (y_p, y_s, wkv_p, shift_p, ret_p, wkv_s, shift_s, ret_s)


KAPPA = float(np.exp(-0.5))
RMS_EPS = 1e-6
GN_EPS_A = 64 * 1e-5
GN_EPS_B = 1e-5
ARENA_BYTES = 212480


class Ctx:
    def send_view(self, r0, t0, n):
        i = min(t0 // TH, 2)
        o = t0 - i * TH
        return self.send[i][r0:r0 + 128, o:o + n]


def dram_in(nc, name, shape, dt=F32):
    return Tile(nc.dram_tensor(name, list(shape), dt, kind="ExternalInput").ap(), name)


def dram_out(nc, name, shape, dt=F32):
    return Tile(nc.dram_tensor(name, list(shape), dt, kind="ExternalOutput").ap(), name)


def setup(nc, es, stage, dbg_shape=None):
    C = Ctx()
    C.nc = nc
    C.S = Sched(nc)
    S = C.S
    I = C.I = {}
    I["xTp"] = dram_in(nc, "xTp", [8, 128, 16 * 256])
    I["xTs"] = dram_in(nc, "xTs", [128, 16 * NS])
    I["w1"] = dram_in(nc, "w1", [33, 128, 2048])
    I["w2"] = dram_in(nc, "w2", [32, 128, 2048])
    I["w3"] = dram_in(nc, "w3", [16, 128, 256])
    I["wB"] = dram_in(nc, "wB", [128, 512])
    I["pc"] = dram_in(nc, "pc", [128, NPC])
    I["cst"] = dram_in(nc, "cst", [128, NCST])
    I["rot"] = dram_in(nc, "rot", [128, 2, NT])
    I["xres"] = dram_in(nc, "xres", [D, NTH])
    I["pT"] = dram_in(nc, "pT", [256, NTH])
    I["shs"] = dram_in(nc, "shs", [13, 128, 32])
    I["wkv_in"] = dram_in(nc, "wkv_in", [4, 128, 32, 64])
    I["ret_in"] = dram_in(nc, "ret_in", [2, 32, 2, 128, 256])
    I["sel"] = dram_in(nc, "sel", [128, 2])
    O = C.O = {}
    O["yT"] = dram_out(nc, "yT", [D, NTH])
    O["shp"] = dram_out(nc, "shp", [128, 16])
    O["shs_o"] = dram_out(nc, "shs_o", [13, 128, 32])
    O["wkv_p"] = dram_out(nc, "wkv_p", [4, 128, 64])
    O["wkv_s"] = dram_out(nc, "wkv_s", [4, 128, 32, 64])
    O["ret_p"] = dram_out(nc, "ret_p", [2, 2, 128, 256])
    O["ret_s"] = dram_out(nc, "ret_s", [2, 32, 2, 128, 256])
    if dbg_shape is not None:
        O["dbg"] = dram_out(nc, "dbg", dbg_shape)
    C.send = [Tile(nc.dram_tensor(f"send{i}", [1024, w], BF16).ap(), f"send{i}") for i, w in enumerate((TH, TH, NS))]
    C.recv = [Tile(nc.dram_tensor(f"recv{i}", [2048, w], BF16).ap(), f"recv{i}") for i, w in enumerate((TH, TH, NS))]
    arena_t = es.enter_context(nc.sbuf_tensor("arena", [128, ARENA_BYTES], U8))
    C.A = Arena(arena_t[:, :], ARENA_BYTES, "sb")
    C.ps = [Tile(es.enter_context(nc.psum_tensor(f"ps{i}", [128, 512], F32))[:, :], f"ps{i}") for i in range(8)]
    A = C.A
    C.cst = A.tile([128, NCST], F32, "cst")
    C.pc = A.tile([128, NPC], F32, "pc")
    C.omu = A.tile([128, 13], F32, "omu")
    C.identbf = A.tile([128, 128], BF16, "identbf")
    C.onesbf = A.tile([128, 128], BF16, "onesbf")
    C.bonesbf = A.tile([128, 128], BF16, "bonesbf")
    C.bones64 = A.tile([128, 128], F32, "bones64")
    C.ones256 = A.tile([128, 128], F32, "ones256")
    C.sel = A.tile([128, 2], F32, "sel")
    S.dma("sp", C.cst, I["cst"], stream="c")
    S.dma("sp", C.pc, I["pc"], stream="c")
    S.dma("sp", C.sel, I["sel"], stream="c")
    S.copy(C.identbf, C.cst[:, C_ID:C_ID + 128], eng="dve")
    S.memset(C.onesbf, 1.0, eng="pool")
    S.copy(C.bonesbf, C.cst[:, C_BONES:C_BONES + 128], eng="dve")
    S.ts(C.bones64, C.cst[:, C_BONES:C_BONES + 128], 1.0 / 64.0, None, ALU.mult, eng="dve")
    S.memset(C.ones256, 1.0 / 256.0, eng="pool")
    S.ts(C.omu, C.pc[:, 32:45], -1.0, 1.0, ALU.mult, ALU.add, eng="dve")
    C.ident = C.cst[:, C_ID:C_ID + 128]
    return C


def phase0_rmsnorm(C):
    S, A, I = C.S, C.A, C.I
    C.uT = A.tile([128, 16, NT], BF16, "uT")
    A.mark()
    TW = 256
    NB = 4
    xb = [A.tile([128, 16, TW], F32, f"xb{i}") for i in range(NB)]
    sq = [A.tile([128, 16, TW], BF16, f"sq{i}") for i in range(NB)]
    rt = [A.tile([128, TW], F32, f"rt{i}") for i in range(NB)]
    tiles = [(t0, TW) for t0 in range(0, T, TW)] + [(T, NS)]
    for it, (t0, n) in enumerate(tiles):
        b = it % NB
        src = I["xTp"][it].re("p (kc t) -> p kc t", kc=16) if it < 8 else I["xTs"].re("p (kc t) -> p kc t", kc=16)
        S.dma("sp", xb[b][:, :, 0:n], src, stream=f"x{b}")
        S.act(sq[b][:, :, 0:n], xb[b][:, :, 0:n], AF.Square)
        ps = C.ps[it % 2]
        for kc in range(16):
            S.mm(ps[:, 0:n], C.onesbf, sq[b][:, kc, 0:n], start=(kc == 0), stop=(kc == 15))
        S.act(rt[b][:, 0:n], ps[:, 0:n], AF.Sqrt, bias=RMS_EPS, scale=1.0 / D)
        S.recip(rt[b][:, 0:n], rt[b][:, 0:n])
        for kc in range(16):
            S.stt(C.uT[:, kc, t0:t0 + n], xb[b][:, kc, 0:n], C.pc[:, kc:kc + 1], rt[b][:, 0:n], ALU.mult, ALU.mult)
    A.release()


MSTOP = os.environ.get("MSTOP")

NEG = ALU.mult


def phase1_alloc(C):
    A = C.A
    C.wf = [A.tile([128, 16, 128], F32, f"wf{i}") for i in range(2)]
    C.wbf = [A.tile([128, 16, 128], BF16, f"wbf{i}") for i in range(4)]
    C.wld = 0
    C.wBf = A.tile([128, 512], F32, "wBf")
    C.wB = A.tile([128, 512], BF16, "wB")
    C.wa_act = A.tile([128, NT], BF16, "wa_act")
    C.shp_sb = A.tile([128, 16], F32, "shp_sb")
    S = C.S
    S.dma("sp", C.wBf, C.I["wB"], stream="c")
    S.copy(C.wB, C.wBf, eng="pool")


def load_w(C, ch, slot, src="w1"):
    S = C.S
    b = C.wld % len(C.wf)
    C.wld += 1
    S.dma("sp", C.wf[b], C.I[src][ch].re("p (kc n) -> p kc n", kc=16), stream=f"w{b}")
    S.copy(C.wbf[slot][:, 0:8, :], C.wf[b][:, 0:8, :], eng="act")
    S.copy(C.wbf[slot][:, 8:16, :], C.wf[b][:, 8:16, :], eng="dve")


def inproj(C, ps, slot, t0, n):
    S = C.S
    for kc in range(16):
        S.mm(ps[:, 0:n], C.wbf[slot][:, kc, :], C.uT[:, kc, t0:t0 + n], start=(kc == 0), stop=(kc == 15))


def shifted(C, ps, zst, s, tt, n, xs, tmp, sample_prev=None):
    S = C.S
    mu = C.pc[:, 32 + s:33 + s]
    omu = C.omu[:, s:s + 1]
    if sample_prev is None:
        if tt == 0:
            S.memset(zst[:, 0:1], 0.0, eng="pool")
        else:
            S.copy(zst[:, 0:1], zst[:, 512:513], eng="act")
        S.copy(zst[:, 1:1 + n], ps[:, 0:n], eng="act")
        S.ts(tmp[:, 0:n], zst[:, 0:n], mu, None, ALU.mult)
        S.stt(xs[:, 0:n], zst[:, 1:1 + n], omu, tmp[:, 0:n], ALU.mult, ALU.add)
        if tt == 3:
            S.copy(C.shp_sb[:, s:s + 1], zst[:, 512:513], eng="act")
    else:
        zs = sample_prev["z"]
        S.copy(zs[:, 0:n], ps[:, 0:n], eng="act")
        S.dma("pool", C.O["shs_o"][s], zs[:, 0:n], stream="po")
        S.ts(tmp[:, 0:n], sample_prev["prev"][:, 0:n], mu, None, ALU.mult)
        S.stt(xs[:, 0:n], zs[:, 0:n], omu, tmp[:, 0:n], ALU.mult, ALU.add)


def phase1a_wa(C):
    S, A = C.S, C.A
    A.mark()
    zst = A.tile([128, 513], F32, "zst_wa")
    xs = A.tile([128, 512], F32, "xs_wa")
    tmp = A.tile([128, 512], F32, "tmp_wa")
    zs = A.tile([128, 32], F32, "zs_wa")
    prev = A.tile([128, 32], F32, "prev_wa")
    load_w(C, 0, 0)
    S.dma("sp", prev, C.I["shs"][0], stream="c")
    for tt in range(5):
        t0, n = (tt * 512, 512) if tt < 4 else (T, NS)
        ps = C.ps[tt % 2]
        inproj(C, ps, 0, t0, n)
        shifted(C, ps, zst, 0, tt, n, xs, tmp, None if tt < 4 else {"z": zs, "prev": prev})
        S.act(C.wa_act[0:64, t0:t0 + n], xs[0:64, 0:n], AF.Tanh)
        S.copy(C.wa_act[64:128, t0:t0 + n], xs[64:128, 0:n], eng="dve")
    A.release()


def rwkv_alloc(C):
    A = C.A
    R = Ctx2()
    f = lambda name, w=512: A.tile([128, w], F32, name)
    R.zst = [A.tile([128, 513], F32, f"zst{i}") for i in range(3)]
    R.t = {k: f(k) for k in ("sg", "a", "cp", "d", "eC", "eNC", "eNCp", "eCC", "xk", "kkn", "kp", "bv", "xr", "xv", "t1", "t2")}
    R.kk2 = A.tile([128, 512], BF16, "kk2")
    R.ARt = [A.tile([128, 4, 256], BF16, f"ARt{i}") for i in range(2)]
    R.BtT = [A.tile([128, 512], BF16, f"BtT{i}") for i in range(2)]
    R.KtT = [A.tile([128, 512], BF16, f"KtT{i}") for i in range(2)]
    R.Bh = A.tile([128, 512], BF16, "Bh")
    R.Kh = A.tile([128, 512], BF16, "Kh")
    R.Vf = A.tile([128, 512], BF16, "Vf")
    R.Btm = A.tile([128, 4, 128], BF16, "Btm")
    R.Ktm = A.tile([128, 4, 128], BF16, "Ktm")
    R.Vtm = A.tile([128, 4, 128], BF16, "Vtm")
    R.bonus = [A.tile([128, 512], F32, f"bonus{i}") for i in range(2)]
    R.gate = [A.tile([128, 512], BF16, f"gate{i}") for i in range(2)]
    R.wC = [A.tile([128, 4], F32, f"wC{i}") for i in range(2)]
    R.QA = [A.tile([128, 4, 256], BF16, f"QA{h}") for h in range(2)]
    R.KA = [A.tile([128, 4, 256], BF16, f"KA{h}") for h in range(2)]
    R.P = [[A.tile([128, 4, 128], BF16, f"P{b}_{h}") for h in range(2)] for b in range(2)]
    R.Q = [[A.tile([128, 4, 128], BF16, f"Q{b}_{h}") for h in range(2)] for b in range(2)]
    R.Xb = [[A.tile([128, 4, 128], BF16, f"Xb{b}_{h}") for h in range(2)] for b in range(2)]
    R.X7 = [A.tile([128, 4, 128], BF16, f"X7_{h}") for h in range(2)]
    R.McT = [A.tile([128, 128], F32, f"McT{c}") for c in range(4)]
    R.Sbd = [A.tile([128, 128], BF16, f"Sbd{i}") for i in range(2)]
    R.QpT = [A.tile([128, 128], BF16, f"QpT{c}") for c in range(4)]
    R.Sf = [A.tile([128, 64], F32, f"Sf{i}") for i in range(2)]
    R.yT = A.tile([128, 512], F32, "yT")
    R.mixo = A.tile([128, 512], BF16, "mixo")
    R.s = {k: A.tile([128, 32], F32, "s_" + k) for k in ("z0", "z1", "z2", "p0", "p1", "p2", "w", "na", "bv", "kp", "r", "v")}
    v3 = lambda k: R.t[k][:, :].re("p (n v) -> p n v", n=8)
    R.ST, R.st1, R.st2, R.Sn = v3("eC"), v3("eCC"), v3("cp"), v3("d")
    v4 = lambda k: R.t[k][:, :].re("p (n v) -> p n v", n=4)
    R.abd = [v4("eNC"), v4("eNCp")]
    R.vbd = [v4("xk"), v4("kkn")]
    R.Snbd = [v4("xr"), v4("xv")]
    return R


class Ctx2:
    pass


def rwkv_prep_tile(C, R, j, tt, par=0):
    S = C.S
    t = R.t
    t0, n = (tt * 512, 512) if tt < 4 else (T, NS)
    pcj = lambda q: C.pc[:, 58 + 7 * j + q:59 + 7 * j + q]
    psA, psB, psM = C.ps[0], C.ps[0], C.ps[1]
    ARt, BtT, KtT, wC, bonus, gate = R.ARt[par], R.BtT[par], R.KtT[par], R.wC[par], R.bonus[par], R.gate[par]
    S.mm(psM[:, 0:n], C.wB[0:64, 128 * j:128 * j + 128], C.wa_act[0:64, t0:t0 + n])
    S.act(t["sg"][:, 0:n], psM[:, 0:n], AF.Sigmoid, bias=pcj(0))
    S.mm(psM[:, 0:n], C.wB[64:128, 128 * j:128 * j + 128], C.wa_act[64:128, t0:t0 + n])
    S.act(t["a"][:, 0:n], psM[:, 0:n], AF.Sigmoid, bias=pcj(1))
    yield
    samp = None
    if tt < 4:
        S.op("dve", lambda e, o=t["cp"].ap, m=C.cst.ap[:, C_RESET:C_RESET + 512], s=t["sg"].ap:
             e.tensor_tensor_scan(out=o, data0=m, data1=s, initial=0.0, op0=ALU.mult, op1=ALU.add),
             ins=(C.cst, t["sg"]), outs=(t["cp"],))
        S.act(t["eC"], t["cp"], AF.Exp, scale=KAPPA)
        S.act(t["eNC"], t["cp"], AF.Exp, scale=-KAPPA)
        S.tt(t["d"], t["cp"], t["sg"], ALU.subtract)
        S.act(t["eNCp"], t["d"], AF.Exp, scale=-KAPPA)
        cend = t["cp"][:, :].re("p (c t) -> p c t", t=128)[:, :, 127:128]
        S.tt(t["d"][:, :].re("p (c t) -> p c t", t=128), t["cp"][:, :].re("p (c t) -> p c t", t=128),
             cend.bc([128, 4, 128]), ALU.subtract)
        S.act(t["eCC"], t["d"], AF.Exp, scale=KAPPA)
        S.act(wC[:, :].un(2), cend, AF.Exp, scale=-KAPPA)
    yield
    inproj(C, psA, 0, t0, n)
    if tt == 4:
        samp = {"z": R.s["z0"], "prev": R.s["p0"]}
    shifted(C, psA, R.zst[0], 1 + 3 * j, tt, n, t["xk"], t["t1"], samp)
    S.act(R.kk2[:, 0:n], t["xk"][:, 0:n], AF.Square, scale=pcj(2))
    yield
    inproj(C, psB, 1, t0, n)
    if tt == 4:
        samp = {"z": R.s["z1"], "prev": R.s["p1"]}
    shifted(C, psB, R.zst[1], 2 + 3 * j, tt, n, t["xr"], t["t1"], samp)
    S.mm(psM[:, 0:n], C.bonesbf, R.kk2[:, 0:n])
    yield
    S.act(t["t1"][:, 0:n], psM[:, 0:n], AF.Sqrt)
    S.ts(t["t1"][:, 0:n], t["t1"][:, 0:n], 1e-12, None, ALU.max)
    S.recip(t["t1"][:, 0:n], t["t1"][:, 0:n])
    S.stt(t["kkn"][:, 0:n], t["xk"][:, 0:n], pcj(2), t["t1"][:, 0:n], ALU.mult, ALU.mult)
    S.ts(t["t2"][:, 0:n], t["a"][:, 0:n], -1.0, pcj(3), ALU.add, ALU.mult)
    S.stt(t["kp"][:, 0:n], t["t2"][:, 0:n], 1.0, t["xk"][:, 0:n], ALU.add, ALU.mult)
    S.tt(t["bv"][:, 0:n], t["kkn"][:, 0:n], t["a"][:, 0:n], ALU.mult)
    yield
    if tt < 4:
        r3 = lambda v: v[:, :].re("p (c t) -> p c t", t=128)
        S.stt(ARt[:, :, 0:128], r3(t["kkn"]), -1.0, r3(t["eNCp"]), ALU.mult, ALU.mult)
        S.tt(BtT, t["bv"], t["eC"], ALU.mult)
        S.tt(R.Bh, t["bv"], t["eCC"], ALU.mult)
        S.tt(KtT, t["kp"], t["eC"], ALU.mult)
        S.tt(R.Kh, t["kp"], t["eCC"], ALU.mult)
    else:
        S.act(R.s["w"], t["sg"][:, 0:n], AF.Exp, scale=-KAPPA)
        S.ts(R.s["na"], t["kkn"][:, 0:n], -1.0, None, ALU.mult)
        S.copy(R.s["bv"], t["bv"][:, 0:n])
        S.copy(R.s["kp"], t["kp"][:, 0:n])
    if tt < 4:
        S.tt(ARt[:, :, 128:256], r3(t["xr"]), r3(t["eNC"]), ALU.mult)
    else:
        S.copy(R.s["r"], t["xr"][:, 0:n])
    S.stt(R.kk2[:, 0:n], t["xr"][:, 0:n], pcj(4), t["kp"][:, 0:n], ALU.mult, ALU.mult)
    yield
    inproj(C, psA, 2, t0, n)
    S.mm(psM[:, 0:n], C.bonesbf, R.kk2[:, 0:n])
    if tt == 4:
        samp = {"z": R.s["z2"], "prev": R.s["p2"]}
    shifted(C, psA, R.zst[2], 3 + 3 * j, tt, n, t["xv"], t["t1"], samp)
    S.tt(bonus[:, 0:n], psM[:, 0:n], t["xv"][:, 0:n], ALU.mult)
    if tt < 4:
        S.copy(R.Vf, t["xv"], eng="act")
    else:
        S.copy(R.s["v"], t["xv"][:, 0:n])
    yield
    inproj(C, psB, 3, t0, n)
    S.act(gate[:, 0:n], psB[:, 0:n], AF.Silu)
    yield


def rwkv_transposes(C, R):
    S = C.S
    for src, dst, pb in ((R.Bh, R.Btm, C.ps[3]), (R.Kh, R.Ktm, C.ps[4]), (R.Vf, R.Vtm, C.ps[5])):
        for c in range(4):
            S.mm(pb[:, 128 * c:128 * c + 128], src[:, 128 * c:128 * c + 128], C.identbf)
        S.copy(dst[:, :, :], pb[:, :].re("p (c f) -> p c f", c=4), eng="act")


def rwkv_machinery(C, R, first_batch, par=0):
    S = C.S
    ps = C.ps
    M1 = C.cst[:, C_M1:C_M1 + 256]
    MSL = C.cst[:, C_MSL:C_MSL + 128]
    hs = lambda h: slice(64 * h, 64 * h + 64)
    ARt, BtT_, KtT_, wC = R.ARt[par], R.BtT[par], R.KtT[par], R.wC[par]
    v4 = lambda bank: bank[:, :].re("p (c f) -> p c f", c=4)
    for (src, dst) in ((BtT_, R.QA), (KtT_, R.KA)):
        for h in range(2):
            for cl in range(4):
                pb = ps[4 + 2 * h + cl // 2]
                S.mm(pb[:, 256 * (cl % 2):256 * (cl % 2) + 256], src[hs(h), 128 * cl:128 * cl + 128], ARt[hs(h), cl, :])
        for h in range(2):
            for half in range(2):
                pb = ps[4 + 2 * h + half]
                S.tt(dst[h][:, 2 * half:2 * half + 2, :], pb[:, :].re("p (c f) -> p c f", c=2),
                     M1.un(1).bc([128, 2, 256]), ALU.mult)
        yield
    for h in range(2):
        for cl in range(4):
            S.mm(ps[4 + h][:, 128 * cl:128 * cl + 128], ARt[hs(h), cl, 0:128], BtT_[hs(h), 128 * cl:128 * cl + 128])
    for h in range(2):
        S.tt(R.P[0][h], v4(ps[4 + h]), MSL.un(1).bc([128, 4, 128]), ALU.mult)
    yield
    for h in range(2):
        pb = ps[2 + h]
        for cl in range(4):
            o = 128 * cl
            S.mm(pb[:, o:o + 64], ARt[hs(h), cl, 0:128], C.identbf[hs(h), hs(h)], start=(cl == 0), stop=False,
                 skip_group_check=True)
            S.mm(pb[:, o + 64:o + 128], R.KA[h][:, cl, 0:128], R.Vtm[:, cl, hs(h)], start=False, stop=False,
                 skip_group_check=True)
    for h in range(2):
        S.copy(R.Xb[0][h], v4(ps[2 + h]), eng=("act" if h else "dve"))
    yield
    for k in range(7):
        cur, nxt = k % 2, (k + 1) % 2
        Qk = (lambda h, cl: R.QA[h][:, cl, 0:128]) if k == 0 else (lambda h, cl, cur=cur: R.Q[cur][h][:, cl, :])
        for h in range(2):
            for cl in range(4):
                o = 128 * cl
                if k < 5:
                    S.mm(ps[4 + h][:, o:o + 128], Qk(h, cl), R.P[cur][h][:, cl, :])
                S.mm(ps[2 + h][:, o:o + 128], Qk(h, cl), R.Xb[cur][h][:, cl, :], start=False, stop=(k == 6),
                     skip_group_check=True)
                if k < 6:
                    S.mm(ps[6 + h][:, o:o + 128], R.P[cur][h][:, cl, :], Qk(h, cl))
        yield
        for h in range(2):
            if k < 5:
                S.copy(R.P[nxt][h], v4(ps[4 + h]), eng="act")
            if k < 6:
                S.copy(R.Q[nxt][h], v4(ps[6 + h]), eng="dve")
                S.copy(R.Xb[nxt][h], v4(ps[2 + h]), eng=("act" if h else "dve"))
            else:
                S.copy(R.X7[h], v4(ps[2 + h]), eng=("act" if h else "dve"))
        yield
    for h in range(2):
        for cl in range(4):
            S.mm(ps[4][hs(h), 64 * cl:64 * cl + 64], R.X7[h][:, cl, 0:64], R.Btm[:, cl, hs(h)])
            S.mm(ps[5][hs(h), 128 * cl:128 * cl + 128], R.X7[h][:, cl, 0:64], R.QA[h][:, cl, 128:256])
    for cl in range(4):
        if first_batch:
            S.memset(R.McT[cl], 0.0, eng="pool")
        for h in range(2):
            S.stt(R.McT[cl][hs(h), hs(h)], C.cst[hs(h), C_ID64:C_ID64 + 64], wC[hs(h), cl:cl + 1],
                  ps[4][hs(h), 64 * cl:64 * cl + 64], ALU.mult, ALU.add)
        S.tt(R.QpT[cl], ps[5][:, 128 * cl:128 * cl + 128], ARt[:, cl, 128:256], ALU.add)
    yield
    if first_batch:
        S.memset(R.Sf[0], 0.0, eng="pool")
        for i in range(2):
            S.memset(R.Sbd[i], 0.0, eng="pool")
        R.si = 0
    for cl in range(4):
        si, so = R.si, 1 - R.si
        yo = ps[6][:, 128 * cl:128 * cl + 128]
        so_ = ps[7][:, 64 * cl:64 * cl + 64]
        for h in range(2):
            S.mm(yo[hs(h), :], R.X7[h][:, cl, 64:128], R.QA[h][:, cl, 128:256], start=True, stop=False)
            S.mm(yo[hs(h), :], R.Vtm[:, cl, hs(h)], R.KA[h][:, cl, 128:256], start=False, stop=False)
            S.mm(so_[hs(h), :], R.Btm[:, cl, hs(h)], R.X7[h][:, cl, 64:128], start=True, stop=False)
            S.mm(so_[hs(h), :], R.Ktm[:, cl, hs(h)], R.Vtm[:, cl, hs(h)], start=False, stop=False)
        S.mm(yo, R.Sbd[si], R.QpT[cl], start=False, stop=True)
        S.mm(so_, R.McT[cl], R.Sf[si], start=False, stop=True)
        S.copy(R.Sf[so], so_, eng="dve")
        for h in range(2):
            S.copy(R.Sbd[so][hs(h), hs(h)], so_[hs(h), :], eng="dve")
        R.si = so
        yield
    S.copy(R.yT, ps[6], eng="act")
    yield


def groupnorm_a(C, R, j, n, yT, t0, par=0):
    S = C.S
    t = R.t
    pcj = lambda q: C.pc[:, 58 + 7 * j + q:59 + 7 * j + q]
    pm, pq = C.ps[0], C.ps[1]
    S.act(t["t1"][:, 0:n], yT[:, 0:n], AF.Square)
    S.mm(pm[:, 0:n], C.bones64, yT[:, 0:n])
    S.mm(pq[:, 0:n], C.bones64, t["t1"][:, 0:n])
    S.copy(t["t2"][:, 0:n], pm[:, 0:n], eng="act")
    S.tt(t["t1"][:, 0:n], t["t2"][:, 0:n], t["t2"][:, 0:n], ALU.mult)
    S.tt(t["t1"][:, 0:n], pq[:, 0:n], t["t1"][:, 0:n], ALU.subtract)
    S.act(t["t1"][:, 0:n], t["t1"][:, 0:n], AF.Sqrt, bias=GN_EPS_A)
    S.recip(t["t1"][:, 0:n], t["t1"][:, 0:n])
    S.tt(t["t2"][:, 0:n], yT[:, 0:n], t["t2"][:, 0:n], ALU.subtract)
    S.tt(t["t2"][:, 0:n], t["t2"][:, 0:n], t["t1"][:, 0:n], ALU.mult)
    S.ts(t["t2"][:, 0:n], t["t2"][:, 0:n], pcj(5), pcj(6), ALU.mult, ALU.add)
    S.tt(t["t2"][:, 0:n], t["t2"][:, 0:n], R.bonus[par][:, 0:n], ALU.add)
    S.tt(R.mixo[:, 0:n], t["t2"][:, 0:n], R.gate[par][:, 0:n], ALU.mult)
    S.dma("pool", C.send_view(128 * j, t0, n), R.mixo[:, 0:n], stream="pm")


def rwkv_sample(C, R, j):
    S = C.S
    ps = C.ps
    hs = lambda h: slice(64 * h, 64 * h + 64)
    id64 = C.cst[:, C_ID64:C_ID64 + 64]
    for i in range(2):
        S.memset(R.abd[i], 0.0, eng="pool")
        S.memset(R.vbd[i], 0.0, eng="pool")
        S.memset(R.Snbd[i], 0.0, eng="pool")
    STb = [R.ST, R.t["sg"][:, :].re("p (n v) -> p n v", n=8)]

    def stage_a(g):
        n0 = 8 * g
        ST = STb[g % 2]
        S.dma("sp", ST, C.I["wkv_in"][j][:, n0:n0 + 8, :], stream=f"st{g % 2}")
        for half in range(2):
            m0 = n0 + 4 * half
            for h in range(2):
                S.copy(R.abd[half][hs(h), :, hs(h)], R.s["na"][hs(h), m0:m0 + 4].un(2).bc([64, 4, 64]))
                S.copy(R.vbd[half][hs(h), :, hs(h)], R.s["v"][hs(h), m0:m0 + 4].un(2).bc([64, 4, 64]))
            pb = ps[2 + 2 * (g % 2) + half]
            for nl in range(4):
                o = 128 * nl
                S.mm(pb[:, o:o + 64], R.abd[half][:, nl, :], ST[:, 4 * half + nl, :])
                S.mm(pb[:, o + 64:o + 128], R.vbd[half][:, nl, :], id64)

    def stage_b(g):
        n0 = 8 * g
        ST = STb[g % 2]
        bc = lambda k: R.s[k][:, n0:n0 + 8].un(2).bc([128, 8, 64])
        S.tt(R.st1, ST, bc("w"), ALU.mult)
        for half in range(2):
            sl = slice(4 * half, 4 * half + 4)
            pv = ps[2 + 2 * (g % 2) + half][:, :].re("p (n f) -> p n f", n=4)
            bch = lambda k: R.s[k][:, n0 + 4 * half:n0 + 4 * half + 4].un(2).bc([128, 4, 64])
            S.tt(R.st2[:, sl, :], pv[:, :, 0:64], bch("bv"), ALU.mult)
            S.tt(R.Sn[:, sl, :], pv[:, :, 64:128], bch("kp"), ALU.mult)
        S.tt(R.st1, R.st1, R.st2, ALU.add)
        S.tt(R.Sn, R.Sn, R.st1, ALU.add)
        S.dma("pool", C.O["wkv_s"][j][:, n0:n0 + 8, :], R.Sn, stream="po")
        for half in range(2):
            for h in range(2):
                S.copy(R.Snbd[half][hs(h), :, hs(h)], R.Sn[hs(h), 4 * half:4 * half + 4, :])
            for nl in range(4):
                n = n0 + 4 * half + nl
                S.mm(ps[0][:, n:n + 1], R.Snbd[half][:, nl, :], R.s["r"][:, n:n + 1])

    stage_a(0)
    for g in range(4):
        if g + 1 < 4:
            stage_a(g + 1)
        stage_b(g)
    S.copy(R.yT[:, 0:32], ps[0][:, 0:32], eng="act")


def _drain(g):
    for _ in g:
        pass


def _interleave(gm, gp):
    STOP = object()
    am = ap = True
    while am or ap:
        if am:
            for _ in range(2):
                if next(gm, STOP) is STOP:
                    am = False
                    break
        if ap:
            if gp is None or next(gp, STOP) is STOP:
                ap = False


def phase1b_rwkv(C, pairs=range(4), stop_after=None):
    S, A = C.S, C.A
    A.mark()
    R = rwkv_alloc(C)
    C.R = R
    pairs = list(pairs)

    def load_pair(j):
        for q in range(4):
            load_w(C, 1 + 4 * j + q, q)
        for q in range(3):
            S.dma("sp", R.s[f"p{q}"], C.I["shs"][1 + 3 * j + q], stream=f"sh{q}")
    if pairs:
        load_pair(pairs[0])
    for ip, j in enumerate(pairs):
        _drain(rwkv_prep_tile(C, R, j, 0, 0))
        rwkv_transposes(C, R)
        for tt in range(4):
            gm = rwkv_machinery(C, R, tt == 0, tt % 2)
            gp = rwkv_prep_tile(C, R, j, tt + 1, (tt + 1) % 2) if tt < 3 else None
            _interleave(gm, gp)
            if tt < 3:
                rwkv_transposes(C, R)
            groupnorm_a(C, R, j, 512, R.yT, tt * 512, tt % 2)
        S.dma("pool", C.O["wkv_p"][j], R.Sf[R.si], stream="po")
        _drain(rwkv_prep_tile(C, R, j, 4, 0))
        if ip + 1 < len(pairs):
            load_pair(pairs[ip + 1])
        rwkv_sample(C, R, j)
        groupnorm_a(C, R, j, 32, R.yT, T, 0)
    A.release()
    return R


def groupnorm_b(C, B, hb, n, t0):
    S = C.S
    pm, pq = C.ps[6], C.ps[7]
    S.act(B.sq[:, :, 0:n], B.yB[:, :, 0:n], AF.Square)
    for vc in range(2):
        S.mm(pm[:, 0:n], C.ones256, B.yB[:, vc, 0:n], start=(vc == 0), stop=(vc == 1))
    for vc in range(2):
        S.mm(pq[:, 0:n], C.ones256, B.sq[:, vc, 0:n], start=(vc == 0), stop=(vc == 1))
    S.copy(B.g1[:, 0:n], pm[:, 0:n], eng="act")
    S.tt(B.g2[:, 0:n], B.g1[:, 0:n], B.g1[:, 0:n], ALU.mult)
    S.tt(B.g2[:, 0:n], pq[:, 0:n], B.g2[:, 0:n], ALU.subtract)
    S.act(B.g2[:, 0:n], B.g2[:, 0:n], AF.Sqrt, bias=GN_EPS_B)
    S.recip(B.g2[:, 0:n], B.g2[:, 0:n])
    for vc in range(2):
        q = 86 + 2 * (2 * hb + vc)
        S.tt(B.g3[:, 0:n], B.yB[:, vc, 0:n], B.g1[:, 0:n], ALU.subtract)
        S.tt(B.g3[:, 0:n], B.g3[:, 0:n], B.g2[:, 0:n], ALU.mult)
        S.ts(B.g3[:, 0:n], B.g3[:, 0:n], C.pc[:, q:q + 1], C.pc[:, q + 1:q + 2], ALU.mult, ALU.add)
        S.tt(B.mixo[:, 0:n], B.g3[:, 0:n], B.gate[:, vc, t0:t0 + n], ALU.mult)
        r0 = 512 + 256 * hb + 128 * vc
        S.dma("pool", C.send_view(r0, t0, n), B.mixo[:, 0:n], stream="pm")


def ret_alloc(C):
    A = C.A
    B = Ctx2()
    B.cs = A.tile([128, 2, 512], F32, "cs")
    B.qrT = A.tile([128, 2, T], BF16, "qrT")
    B.krT = A.tile([128, 2, T], BF16, "krT")
    B.gate = A.tile([128, 2, NT], BF16, "gateB")
    B.Vtm = A.tile([128, 16, 256], BF16, "VtmB")
    B.Ktm = A.tile([128, 16, 256], BF16, "KtmB")
    B.S0b = A.tile([128, 4, 2, 256], F32, "S0b")
    flat = B.S0b.ap.rearrange("p n x v -> p (n x v)")
    B.xa = Tile(flat[:, 0:512], "xa")
    B.xb = Tile(flat[:, 512:1024], "xb")
    B.t = [Tile(flat[:, 1024:1536], "rt0"), Tile(flat[:, 1536:2048], "rt1"), A.tile([128, 512], F32, "rt2"), A.tile([128, 512], F32, "rt3")]
    B.yB = A.tile([128, 2, 512], F32, "yB")
    B.sq = A.tile([128, 2, 512], F32, "sqB")
    B.g1 = A.tile([128, 512], F32, "g1")
    B.g2 = A.tile([128, 512], F32, "g2")
    B.g3 = A.tile([128, 512], F32, "g3")
    B.mixo = A.tile([128, 512], BF16, "mixoB")
    B.Sf = [A.tile([128, 2, 256], F32, f"SfB{i}") for i in range(2)]
    B.Sb = [A.tile([128, 2, 256], BF16, f"SbB{i}") for i in range(2)]
    B.scT = [A.tile([128, 128], BF16, f"scT{i}") for i in range(2)]
    B.qs = A.tile([128, 2, 32], F32, "q_s")
    B.ks = A.tile([128, 2, 32], F32, "k_s")
    B.vs = A.tile([128, 2, 32], F32, "v_s")
    B.S0 = A.tile([128, 4, 2, 256], F32, "S0")
    B.vbc = A.tile([128, 2, 4, 128], F32, "vbcB")
    return B


def rotary(C, B, ps_a, ps_b, hb, tt, n, is_q, dstT, dst_s):
    S = C.S
    t0 = tt * 512 if tt < 4 else T
    cos, sin = B.cs[:, 0, 0:n], B.cs[:, 1, 0:n]
    if is_q and tt < 4:
        gq = C.cst[:, C_GQ + 512 * hb:C_GQ + 512 * hb + 512]
        S.tt(B.xa[:, 0:n], ps_a[:, 0:n], gq[:, 0:n], ALU.mult)
        S.tt(B.xb[:, 0:n], ps_b[:, 0:n], gq[:, 0:n], ALU.mult)
    elif is_q:
        S.copy(B.xa[:, 0:n], ps_a[:, 0:n], eng="act")
        S.copy(B.xb[:, 0:n], ps_b[:, 0:n], eng="act")
    else:
        S.act(B.xa[:, 0:n], ps_a[:, 0:n], AF.Copy, scale=1.0 / 16.0)
        S.act(B.xb[:, 0:n], ps_b[:, 0:n], AF.Copy, scale=1.0 / 16.0)
    t = B.t
    S.tt(t[0][:, 0:n], B.xa[:, 0:n], cos, ALU.mult)
    S.tt(t[1][:, 0:n], B.xb[:, 0:n], sin, ALU.mult)
    S.tt(t[2][:, 0:n], B.xa[:, 0:n], sin, ALU.mult)
    S.tt(t[3][:, 0:n], B.xb[:, 0:n], cos, ALU.mult)
    if tt < 4:
        S.tt(dstT[:, 0, t0:t0 + n], t[0][:, 0:n], t[1][:, 0:n], ALU.subtract)
        S.tt(dstT[:, 1, t0:t0 + n], t[2][:, 0:n], t[3][:, 0:n], ALU.add)
    else:
        S.tt(dst_s[:, 0, :], t[0][:, 0:n], t[1][:, 0:n], ALU.subtract)
        S.tt(dst_s[:, 1, :], t[2][:, 0:n], t[3][:, 0:n], ALU.add)


def ret_head(C, B, hb):
    S = C.S
    ps = C.ps
    base = 17 + 8 * hb
    ident = C.cst[:, C_ID:C_ID + 128]
    for q in range(4):
        load_w(C, base + q, q)
    for tt in range(5):
        t0, n = (tt * 512, 512) if tt < 4 else (T, NS)
        S.dma("sp", B.cs[:, :, 0:n], C.I["rot"][:, :, t0:t0 + n], stream="r")
        inproj(C, ps[0], 0, t0, n)
        inproj(C, ps[1], 1, t0, n)
        rotary(C, B, ps[0], ps[1], hb, tt, n, True, B.qrT, B.qs)
        inproj(C, ps[2], 2, t0, n)
        inproj(C, ps[3], 3, t0, n)
        rotary(C, B, ps[2], ps[3], hb, tt, n, False, B.krT, B.ks)
    for q in range(4):
        load_w(C, base + 4 + q, q)
    for tt in range(5):
        t0, n = (tt * 512, 512) if tt < 4 else (T, NS)
        for vc in range(2):
            inproj(C, ps[vc], vc, t0, n)
            S.act(B.gate[:, vc, t0:t0 + n], ps[vc][:, 0:n], AF.Silu)
    for vc in range(2):
        inproj(C, ps[2 + vc], 2 + vc, T, NS)
        S.copy(B.vs[:, vc, :], ps[2 + vc][:, 0:NS], eng="act")
    for c in range(16):
        pb = ps[c % 2]
        for vc in range(2):
            for kc in range(16):
                S.mm(pb[:, 128 * vc:128 * vc + 128], C.uT[:, kc, 128 * c:128 * c + 128], C.wbf[2 + vc][:, kc, :],
                     start=(kc == 0), stop=(kc == 15), skip_group_check=True)
        S.copy(B.Vtm[:, c, :], pb[:, 0:256], eng=("act" if c % 2 else "dve"))
    for c in range(16):
        pb = ps[2 + c % 2]
        for X in range(2):
            S.mm(pb[:, 128 * X:128 * X + 128], B.krT[:, X, 128 * c:128 * c + 128], C.identbf)
        S.ts(B.Ktm[:, c, :], pb[:, 0:256], C.cst[:, C_KDEC + hb:C_KDEC + hb + 1], None, ALU.mult)
    gam = C.cst[:, C_GAM + hb:C_GAM + hb + 1]
    S0bufs = [B.S0, B.S0b]
    def sample_load(g):
        S.dma("sp", S0bufs[g % 2][:, :, :, :].re("p n x v -> p (n x) v"),
              C.I["ret_in"][hb, 4 * g:4 * g + 4].re("n x p v -> p (n x) v"), stream=f"s{g % 2}")

    def sample_group(g):
        n0 = 4 * g
        B_S0 = S0bufs[g % 2]
        if g + 1 < 8:
            sample_load(g + 1)
        for vc in range(2):
            S.copy(B.vbc[:, vc, :, :], B.vs[:, vc, n0:n0 + 4].un(2).bc([128, 4, 128]))
        S0f = B_S0[:, :, :, :].re("p n x v -> p (n x v)")
        S.ts(S0f, S0f, gam, None, ALU.mult)
        def vbc_mm(nl):
            for vc in range(2):
                S.mm(ps[2 + nl % 2][:, 128 * vc:128 * vc + 128], B.vbc[:, vc, nl, :], ident)
        vbc_mm(0)
        for nl in range(4):
            n = n0 + nl
            pv = ps[2 + nl % 2]
            if nl + 1 < 4:
                vbc_mm(nl + 1)
            for X in range(2):
                S.stt(B_S0[:, nl, X, :], pv[:, 0:256], B.ks[:, X, n:n + 1], B_S0[:, nl, X, :], ALU.mult, ALU.add)
            for vc in range(2):
                for X in range(2):
                    S.mm(ps[4 + vc][:, n:n + 1], B_S0[:, nl, X, 128 * vc:128 * vc + 128], B.qs[:, X, n:n + 1],
                         start=(X == 0), stop=(X == 1), skip_group_check=True)
        S.dma("sp", C.O["ret_s"][hb, n0:n0 + 4].re("n x p v -> p (n x) v"), B_S0[:, :, :, :].re("p n x v -> p (n x) v"), stream=f"so{g % 2}")
    S.barrier()
    sample_load(0)
    S.memset(B.Sf[0], 0.0, eng="pool")
    S.memset(B.Sb[0], 0.0, eng="pool")
    si = 0
    DT = C.cst[:, C_DT + 128 * hb:C_DT + 128 * hb + 128]
    g128 = C.cst[:, C_G128 + hb:C_G128 + hb + 1]
    for c in range(16):
        so = 1 - si
        cs_ = slice(128 * c, 128 * c + 128)
        sc = B.scT[c % 2]
        pS_, pO_ = ps[0], ps[1]
        for X in range(2):
            S.mm(pS_[:, 0:128], B.krT[:, X, cs_], B.qrT[:, X, cs_], start=(X == 0), stop=(X == 1))
        S.tt(sc, pS_[:, 0:128], DT, ALU.mult)
        for X in range(2):
            S.mm(ps[2 + X][:, 0:256], B.Ktm[:, c, 128 * X:128 * X + 128], B.Vtm[:, c, :])
        for vc in range(2):
            po = pO_[:, 128 * vc:128 * vc + 128]
            S.mm(po, B.Vtm[:, c, 128 * vc:128 * vc + 128], sc, start=True, stop=False, skip_group_check=True)
            for X in range(2):
                S.mm(po, B.Sb[si][:, X, 128 * vc:128 * vc + 128], B.qrT[:, X, cs_], start=False, stop=(X == 1),
                     skip_group_check=True)
        S.copy(B.yB[:, :, 128 * (c % 4):128 * (c % 4) + 128], pO_[:, 0:256].re("p (v t) -> p v t", v=2), eng="act")
        for X in range(2):
            S.stt(B.Sf[so][:, X, :], B.Sf[si][:, X, :], g128, ps[2 + X][:, 0:256], ALU.mult, ALU.add)
        S.copy(B.Sb[so], B.Sf[so], eng="act")
        si = so
        if c % 4 == 3:
            groupnorm_b(C, B, hb, 512, 128 * (c - 3))
            if hb == 1 and c == 7:
                exchange(C, (0,))
        if c % 2 == 1:
            sample_group(c // 2)
    S.dma("pool", C.O["ret_p"][hb].re("x p v -> p x v"), B.Sf[si], stream="po")
    for vc in range(2):
        S.copy(B.yB[:, vc, 0:NS], ps[4 + vc][:, 0:NS], eng="act")
    groupnorm_b(C, B, hb, NS, T)
    S.barrier()


def phase1c_retention(C):
    S, A = C.S, C.A
    A.mark()
    B = ret_alloc(C)
    for hb in range(2):
        ret_head(C, B, hb)
    A.release()


def exchange(C, which=(0, 1, 2)):
    S = C.S
    for i in which:
        sb, rb = C.send[i].ap, C.recv[i].ap
        S.custom("pool", lambda e, sb=sb, rb=rb: e.collective_compute(
            "AllGather", ALU.bypass, replica_groups=[[0, 1], [2, 3], [4, 5], [6, 7]], ins=[sb.opt()], outs=[rb.opt()]),
            f"cc{i}", 1, ins=(C.send[i],), outs=(C.recv[i],))


def phase2(C):
    S, A = C.S, C.A
    ps = C.ps
    A.mark()
    C.wf = [A.tile([128, 16, 128], F32, f"wf2_{i}") for i in range(3)]
    C.wbf = [A.tile([128, 16, 128], BF16, f"wbf2_{i}") for i in range(3)]
    h1 = A.tile([128, 16, NTH], F32, "h1")
    h1b = A.tile([128, 16, NTH], BF16, "h1b")
    A.mark()
    mixsel_l = [A.tile([128, NTH], BF16, f"mixsel{kc}") for kc in range(16)]
    cand = [[A.tile([128, TH], BF16, f"cand{i}_{b}") for b in range(2)] for i in range(2)]
    cands = [A.tile([128, NS], BF16, f"cands{i}") for i in range(2)]
    tmpb = [A.tile([128, TH], F32, f"tmpb{i}") for i in range(2)]
    xr = [A.tile([128, NTH], F32, f"xr{i}") for i in range(2)]
    s0, s1 = C.sel[:, 0:1], C.sel[:, 1:2]
    for kc in range(16):
        b = kc % 2
        rows = slice(128 * kc, 128 * kc + 128)
        S.dma("sp", cand[b][0], C.recv[0][rows, :], stream=f"ca{b}")
        S.dma("sp", cand[b][1], C.recv[1][rows, :], stream=f"cb{b}")
        S.dma("sp", cands[b], C.recv[2][rows, 0:NS], stream=f"cs{b}")
        S.ts(tmpb[b], cand[b][0], s0, None, ALU.mult)
        S.stt(mixsel_l[kc][:, 0:TH], cand[b][1], s1, tmpb[b], ALU.mult, ALU.add)
        S.ts(tmpb[b][:, 0:NSH], cands[b][:, 0:NSH], s0, None, ALU.mult)
        S.stt(mixsel_l[kc][:, TH:NTH], cands[b][:, NSH:NS], s1, tmpb[b][:, 0:NSH], ALU.mult, ALU.add)
    tiles = [(0, 512), (512, 512), (1024, NSH)]
    load_w(C, 0, 0, src="w2")
    for dc in range(16):
        if dc + 1 < 16:
            load_w(C, dc + 1, (dc + 1) % 3, src="w2")
        S.dma("sp", xr[dc % 2], C.I["xres"][128 * dc:128 * dc + 128, :], stream=f"xr{dc % 2}")
        for it, (t0, n) in enumerate(tiles):
            pb = ps[(3 * dc + it) % 4]
            for kc in range(16):
                S.mm(pb[:, 0:n], C.wbf[dc % 3][:, kc, :], mixsel_l[kc][:, t0:t0 + n], start=(kc == 0), stop=(kc == 15))
            S.tt(h1[:, dc, t0:t0 + n], pb[:, 0:n], xr[dc % 2][:, t0:t0 + n], ALU.add)
            S.copy(h1b[:, dc, t0:t0 + n], h1[:, dc, t0:t0 + n], eng="act")
    S.barrier()
    A.release()
    pTf = A.tile([128, 2, NTH], F32, "pTf")
    pTb = A.tile([128, 2, NTH], BF16, "pTb")
    wpf = [A.tile([128, 2, 128], F32, f"wpf{i}") for i in range(2)]
    wpb = [A.tile([128, 2, 128], BF16, f"wpb{i}") for i in range(3)]
    sig = [A.tile([128, 512], F32, f"sig{i}") for i in range(2)]
    sq = [A.tile([128, 512], BF16, f"sq2_{i}") for i in range(2)]
    rstd = A.tile([128, NTH], F32, "rstd")
    yo = [A.tile([128, NTH], F32, f"yo{i}") for i in range(4)]
    S.dma("sp", pTf, C.I["pT"].re("(kc p) t -> p kc t", p=128), stream="c")
    S.copy(pTb, pTf, eng="act")
    ssq = [ps[5], ps[6], ps[7]]
    def load_gate(dc):
        load_w(C, 16 + dc, dc % 3, src="w2")
        S.dma("sp", wpf[dc % 2], C.I["w3"][dc].re("p (kc n) -> p kc n", kc=2), stream=f"wp{dc % 2}")
        S.copy(wpb[dc % 3], wpf[dc % 2], eng="act")
    pend = []

    def flush_ssq():
        while pend:
            it_, k_, n_, dc_ = pend.pop(0)
            S.mm(ssq[it_][:, 0:n_], C.onesbf, sq[k_][:, 0:n_], start=(dc_ == 0), stop=(dc_ == 15), skip_group_check=True)
    load_gate(0)
    for dc in range(16):
        if dc + 1 < 16:
            load_gate(dc + 1)
        for it, (t0, n) in enumerate(tiles):
            k = (3 * dc + it) % 2
            pg, pp = ps[(3 * dc + it) % 3], ps[3 + k]
            for kc in range(16):
                S.mm(pg[:, 0:n], C.wbf[dc % 3][:, kc, :], h1b[:, kc, t0:t0 + n], start=(kc == 0), stop=(kc == 15))
            for kc in range(2):
                S.mm(pp[:, 0:n], wpb[dc % 3][:, kc, :], pTb[:, kc, t0:t0 + n], start=(kc == 0), stop=(kc == 1))
            flush_ssq()
            S.act(sig[k][:, 0:n], pg[:, 0:n], AF.Sigmoid)
            S.tt(sig[k][:, 0:n], sig[k][:, 0:n], pp[:, 0:n], ALU.mult)
            S.tt(h1[:, dc, t0:t0 + n], h1[:, dc, t0:t0 + n], sig[k][:, 0:n], ALU.add)
            S.act(sq[k][:, 0:n], h1[:, dc, t0:t0 + n], AF.Square)
            pend.append((it, k, n, dc))
    flush_ssq()
    for it, (t0, n) in enumerate(tiles):
        S.act(rstd[:, t0:t0 + n], ssq[it][:, 0:n], AF.Sqrt, bias=RMS_EPS, scale=1.0 / D)
        S.recip(rstd[:, t0:t0 + n], rstd[:, t0:t0 + n])
    for dc in range(16):
        y = yo[dc % 4]
        for it, (t0, n) in enumerate(tiles):
            S.stt(y[:, t0:t0 + n], h1[:, dc, t0:t0 + n], C.pc[:, 16 + dc:17 + dc], rstd[:, t0:t0 + n], ALU.mult, ALU.mult)
        S.dma("sp", C.O["yT"][128 * dc:128 * dc + 128, :], y, stream=f"y{dc % 4}")
    A.release()


def build(pairs=range(4), do_ret=True, do_p2=True):
    nc = bass.Bass("TRN2", target_bir_lowering=False)
    with ExitStack() as es:
        C = setup(nc, es, 0)
        S, A = C.S, C.A
        A.mark()
        phase0_rmsnorm(C)
        S.barrier()
        phase1_alloc(C)
        phase1a_wa(C)
        S.barrier()
        phase1b_rwkv(C, pairs=pairs)
        S.dma("pool", C.O["shp"], C.shp_sb, stream="po")
        S.barrier()
        if do_ret:
            phase1c_retention(C)
        if do_p2:
            exchange(C, (1, 2))
            S.barrier()
            A.release()
            if os.environ.get("P2ONLYX"):
                t = A.tile([128, 2080], BF16, "xchk")
                S.dma("sp", t[:, 0:1024], C.recv[0][1024:1152, :], stream="c")
                t2 = A.tile([128, 512], F32, "xchk2")
                S.copy(t2, t[:, 0:512])
                S.dma("sp", C.O["yT"][0:128, 0:512], t2, stream="c")
            else:
                phase2(C)
        print("ops", S.nops, {e: len(S.q[e]) for e in ENGS}, flush=True)
        S.emit()
    return nc


_NC = None


def kernel(**inputs):
    global _NC
    per_core = prep_inputs(inputs)
    if _NC is None:
        _NC = build()
    res = run_bass_kernel_spmd(_NC, per_core, core_ids=list(range(8)))
    return assemble(res.results)
```
